# Optimizing a Trainium2 kernel written in Bass

```python
import jax, jax.numpy as jnp
from jax import lax
import numpy as np

D_MODEL = 1024
BATCH = 32
SEQ = 256
DEPTH = 4
DEC_BATCH = 4
DEC_SEQ = 2048
PAST_LEN = 512

GRID_W = 64
N_MIXERS = 2
N_ATTN_LAYERS = (DEPTH + 1) // 2
N_POOL_LAYERS = DEPTH // 2
HEAD_DIM = 128
N_HEADS = D_MODEL // HEAD_DIM
N_KV_HEADS = 2
Q_W = N_HEADS * HEAD_DIM
KV_W = N_KV_HEADS * HEAD_DIM
QKV_W = Q_W + 2 * KV_W
ROPE_PAIRS = HEAD_DIM // 4
ROPE_BASE = 10000.0
Q_BLOCK = 128
POOL_WINDOWS = (2, 4, 8, 16)
N_POOL_GROUPS = len(POOL_WINDOWS)
POOL_DG = D_MODEL // N_POOL_GROUPS
D_FF = ((8 * D_MODEL // 3 + 255) // 256) * 256
N_MOD = 6
EPS = 1e-6

kernel_name = "hybrid_attn_pool_diffusion_step"


def rmsnorm(x, g):
    xf = x.astype(jnp.float32)
    y = xf * lax.rsqrt(jnp.mean(xf * xf, axis=-1, keepdims=True) + EPS)
    return (y * g.astype(jnp.float32)).astype(x.dtype)


def modulation(cond, w, b):
    ada = jax.nn.silu(cond) @ w + b
    return ada.reshape(cond.shape[0], 1, N_MOD, D_MODEL)


def modulate(h, shift, scale):
    return h * (1.0 + scale) + shift


def project_qkv(h, w_qkv, qk_g):
    B, T, _ = h.shape
    qkv = h @ w_qkv
    q = qkv[..., :Q_W].reshape(B, T, N_HEADS, HEAD_DIM)
    k = qkv[..., Q_W:Q_W + KV_W].reshape(B, T, N_KV_HEADS, HEAD_DIM)
    v = qkv[..., Q_W + KV_W:].reshape(B, T, N_KV_HEADS, HEAD_DIM)
    return rmsnorm(q, qk_g[0]), rmsnorm(k, qk_g[1]), v


def axial_rope(x):
    B, n, H, _ = x.shape
    rows = n // GRID_W
    row = jnp.broadcast_to(jnp.arange(rows)[:, None], (rows, GRID_W)).reshape(n).astype(jnp.float32)
    col = jnp.broadcast_to(jnp.arange(GRID_W)[None, :], (rows, GRID_W)).reshape(n).astype(jnp.float32)
    inv = ROPE_BASE ** (-jnp.arange(ROPE_PAIRS, dtype=jnp.float32) / ROPE_PAIRS)
    ang = jnp.stack([row[:, None] * inv, col[:, None] * inv], axis=1)
    cos = jnp.cos(ang)[None, :, None]
    sin = jnp.sin(ang)[None, :, None]
    xf = x.astype(jnp.float32).reshape(B, n, H, 2, 2, ROPE_PAIRS)
    x1, x2 = xf[..., 0, :], xf[..., 1, :]
    out = jnp.stack([x1 * cos - x2 * sin, x2 * cos + x1 * sin], axis=-2)
    return out.reshape(x.shape).astype(x.dtype)


def block_attention(q, k, v):
    B, Tq, H, Dh = q.shape
    kvh = k.shape[2]
    G = H // kvh
    nb = Tq // Q_BLOCK
    scale = 1.0 / np.sqrt(Dh)
    kf = k.astype(jnp.float32)
    qb = q.reshape(B, nb, Q_BLOCK, kvh, G, Dh).transpose(1, 0, 2, 3, 4, 5)

    def one_block(qblk):
        s = jnp.einsum('bqkgd,bskd->bkgqs', qblk.astype(jnp.float32), kf) * scale
        p = jax.nn.softmax(s, axis=-1).astype(v.dtype)
        return jnp.einsum('bkgqs,bskd->bqkgd', p, v)

    o = lax.map(one_block, qb)
    return o.transpose(1, 0, 2, 3, 4, 5).reshape(B, Tq, H * Dh)


def multiscale_pool(h, w_pool, pool_scale):
    B, T, D = h.shape
    hf = h.astype(jnp.float32)
    cs = jnp.concatenate([jnp.zeros((B, 1, D), jnp.float32), jnp.cumsum(hf, axis=1)], axis=1)
    t = jnp.arange(T)
    outs = []
    for g, w in enumerate(POOL_WINDOWS):
        lo = jnp.clip(t - w // 2, 0, T)
        hi = jnp.clip(t + w - w // 2, 0, T)
        cnt = (hi - lo).astype(jnp.float32)[None, :, None]
        csg = cs[..., g * POOL_DG:(g + 1) * POOL_DG]
        pooled = (csg[:, hi] - csg[:, lo]) / cnt - hf[..., g * POOL_DG:(g + 1) * POOL_DG]
        outs.append(pooled.astype(h.dtype) @ w_pool[g])
    return jnp.concatenate(outs, axis=-1) * pool_scale


def swiglu(h, w_up, w_down):
    gu = h @ w_up
    return (jax.nn.silu(gu[..., :D_FF]) * gu[..., D_FF:]) @ w_down


def setup_inputs(seed: int = 0) -> dict:
    key = jax.random.key(seed)
    ks = jax.random.split(key, 16)
    f32 = jnp.float32
    nrm = lambda k, s: jax.random.normal(k, s, f32)
    return {
        "x_prompt": nrm(ks[0], (BATCH, SEQ, D_MODEL)),
        "x_sample": nrm(ks[1], (DEC_BATCH, DEC_SEQ, D_MODEL)),
        "c": nrm(ks[2], (DEC_BATCH, D_MODEL)),
        "cache_k": nrm(ks[3], (DEC_BATCH, N_ATTN_LAYERS, PAST_LEN, N_KV_HEADS, HEAD_DIM)),
        "cache_v": nrm(ks[4], (DEC_BATCH, N_ATTN_LAYERS, PAST_LEN, N_KV_HEADS, HEAD_DIM)),
        "c_ctx": nrm(ks[5], (D_MODEL,)),
        "w_ada": nrm(ks[6], (DEPTH, D_MODEL, N_MOD * D_MODEL)) * (0.5 * D_MODEL ** -0.5),
        "b_ada": nrm(ks[7], (DEPTH, N_MOD * D_MODEL)) * 0.02,
        "norm_gains": 1.0 + 0.05 * nrm(ks[8], (DEPTH, 4, D_MODEL)),
        "w_qkv": nrm(ks[9], (N_ATTN_LAYERS, D_MODEL, QKV_W)) * D_MODEL ** -0.5,
        "qk_gains": 1.0 + 0.05 * nrm(ks[10], (N_ATTN_LAYERS, 2, HEAD_DIM)),
        "w_o": nrm(ks[11], (N_ATTN_LAYERS, Q_W, D_MODEL)) * Q_W ** -0.5,
        "w_pool": nrm(ks[12], (N_POOL_LAYERS, N_POOL_GROUPS, POOL_DG, POOL_DG)) * POOL_DG ** -0.5,
        "pool_scale": 0.5 + 0.1 * nrm(ks[13], (N_POOL_LAYERS, D_MODEL)),
        "w_up": nrm(ks[14], (DEPTH, D_MODEL, 2 * D_FF)) * D_MODEL ** -0.5,
        "w_down": nrm(ks[15], (DEPTH, D_FF, D_MODEL)) * D_FF ** -0.5,
    }


def reference(x_prompt, x_sample, c, cache_k, cache_v, c_ctx, w_ada, b_ada, norm_gains,
              w_qkv, qk_gains, w_o, w_pool, pool_scale, w_up, w_down):
    xp = x_prompt
    xs = x_sample
    new_ks = []
    new_vs = []
    for layer in range(DEPTH):
        g = norm_gains[layer]
        mp = modulation(c_ctx[None, :], w_ada[layer], b_ada[layer])
        ms = modulation(c, w_ada[layer], b_ada[layer])
        hp = modulate(rmsnorm(xp, g[0]), mp[:, :, 0], mp[:, :, 1])
        hs = modulate(rmsnorm(xs, g[0]), ms[:, :, 0], ms[:, :, 1])
        if layer % N_MIXERS == 0:
            a = layer // N_MIXERS
            qp, kp, vp = project_qkv(hp, w_qkv[a], qk_gains[a])
            op = block_attention(qp, kp, vp) @ w_o[a]
            new_ks.append(kp)
            new_vs.append(vp)
            qs, ks_, vs_ = project_qkv(hs, w_qkv[a], qk_gains[a])
            qs = axial_rope(qs)
            ks_ = axial_rope(ks_)
            k_all = jnp.concatenate([ks_, cache_k[:, a]], axis=1)
            v_all = jnp.concatenate([vs_, cache_v[:, a]], axis=1)
            os_ = block_attention(qs, k_all, v_all) @ w_o[a]
        else:
            p = layer // N_MIXERS
            op = multiscale_pool(hp, w_pool[p], pool_scale[p])
            os_ = multiscale_pool(hs, w_pool[p], pool_scale[p])
        xp = xp + mp[:, :, 2] * rmsnorm(op, g[1])
        xs = xs + ms[:, :, 2] * rmsnorm(os_, g[1])
        hp = modulate(rmsnorm(xp, g[2]), mp[:, :, 3], mp[:, :, 4])
        hs = modulate(rmsnorm(xs, g[2]), ms[:, :, 3], ms[:, :, 4])
        xp = xp + mp[:, :, 5] * rmsnorm(swiglu(hp, w_up[layer], w_down[layer]), g[3])
        xs = xs + ms[:, :, 5] * rmsnorm(swiglu(hs, w_up[layer], w_down[layer]), g[3])
    new_k = jnp.stack(new_ks, axis=1)
    new_v = jnp.stack(new_vs, axis=1)
    return (xp, xs, new_k, new_v)
```

```python
import numpy as np
from contextlib import ExitStack
import concourse.bass as bass
import concourse.mybir as mybir
from concourse.bass_utils import run_bass_kernel_spmd

F32 = mybir.dt.float32
BF16 = mybir.dt.bfloat16
AF = mybir.ActivationFunctionType
ALU = mybir.AluOpType

D = 1024
NCH = 8
T = 2048
TB = 512
NTB = 4
DEPTH = 4
DFF = 2816
NF = 22
EPS = 1e-6
PAST = 512
NKT = 20
POOL_W = (2, 4, 8, 16)
ARENA_BYTES = 142848
DEBUG = False
DBG = {}


class Sched:
    ENGS = ("pe", "act", "dve", "pool", "sp")

    def __init__(self, nc, stack):
        self.nc = nc
        self.stack = stack
        self.ops = {e: [] for e in self.ENGS}
        self.count = {e: 0 for e in self.ENGS}
        self.seen = {e: {} for e in self.ENGS}
        self.sems = {}
        for e in self.ENGS:
            self.sems[e] = stack.enter_context(nc.semaphore("s_" + e))
        self.dma_val = {}
        self.res_w = {}
        self.res_r = {}

    def _sem(self, key):
        if key not in self.sems:
            self.sems[key] = self.stack.enter_context(self.nc.semaphore("d_%d" % len(self.sems)))
            self.dma_val[key] = 0
        return self.sems[key]

    def _deps(self, E, reads, writes):
        need = {}

        def add(tok, raw):
            if tok is None:
                return
            k, v = tok
            if k == E and E in ("pe", "sp"):
                return
            if self.seen[E].get(k, 0) >= v:
                return
            if need.get(k, 0) < v:
                need[k] = v

        for r in reads:
            add(self.res_w.get(r), True)
        for w in writes:
            add(self.res_w.get(w), False)
            for k, v in self.res_r.get(w, {}).items():
                add((k, v), False)
        for k, v in need.items():
            self.seen[E][k] = v
        return list(need.items())

    def op(self, E, fns, reads=(), writes=(), phase=True):
        if callable(fns):
            fns = [fns]
        reads = list(reads) + (["phase"] if phase else [])
        waits = self._deps(E, reads, writes)
        self.count[E] += 1
        idx = self.count[E]
        self.ops[E].append((waits, fns, (E, 1)))
        for w in writes:
            self.res_w[w] = (E, idx)
            self.res_r[w] = {}
        for r in reads:
            self.res_r.setdefault(r, {})[E] = idx

    def dma(self, Q, fn, reads=(), writes=(), key=None, phase=True):
        if key is None:
            key = ("dma", (writes[0] if writes else reads[0]))
        self._sem(key)
        reads = list(reads) + (["phase"] if phase else [])
        waits = self._deps(Q, reads, writes)
        self.dma_val[key] += 16
        tok = (key, self.dma_val[key])
        self.ops[Q].append((waits, [fn], (key, 16)))
        for w in writes:
            self.res_w[w] = tok
            self.res_r[w] = {}
        for r in reads:
            self.res_r.setdefault(r, {})[key] = tok[1]

    def final_wait(self, E="sp"):
        waits = []
        for k, v in self.dma_val.items():
            if self.seen[E].get(k, 0) < v:
                waits.append((k, v))
        for e in self.ENGS:
            if e != E and self.count[e] > 0 and self.seen[E].get(e, 0) < self.count[e]:
                waits.append((e, self.count[e]))
        self.ops[E].append((waits, [], None))

    def emit(self):
        nc = self.nc
        with nc.Block() as block:
            def run(E):
                def body(engine):
                    for waits, fns, inc in self.ops[E]:
                        for k, v in waits:
                            engine.wait_ge(self.sems[k], v)
                        last = None
                        for f in fns:
                            last = f(engine)
                        if inc is not None:
                            last.then_inc(self.sems[inc[0]], inc[1])
                return body
            block.tensor(run("pe"))
            block.scalar(run("act"))
            block.vector(run("dve"))
            block.gpsimd(run("pool"))
            block.sync(run("sp"))


class Rot:
    def __init__(self, name, aps):
        self.name = name
        self.aps = aps
        self.i = -1

    def next(self):
        self.i = (self.i + 1) % len(self.aps)
        return self.aps[self.i], (self.name, self.i)


class Arena:
    def __init__(self, ap):
        self.ap = ap
        self.off = 0

    def reset(self):
        self.off = 0

    def f32(self, n):
        assert self.off % 4 == 0
        a = self.ap[:, self.off // 4: self.off // 4 + n]
        self.off += 4 * n
        assert self.off <= ARENA_BYTES, self.off
        return a

    def bf16(self, n):
        assert n % 2 == 0
        a = self.f32(n // 2).bitcast(BF16)
        return a


def build_program(n_layers=DEPTH):
    nc = bass.Bass("TRN2", target_bir_lowering=False)

    def din(name, shape):
        return nc.dram_tensor(name, list(shape), F32, kind="ExternalInput").ap()

    def dout(name, shape):
        return nc.dram_tensor(name, list(shape), F32, kind="ExternalOutput").ap()

    xT_in = din("xT_in", [D, T])
    cond_in = din("cond", [128, 8])
    w_ada = din("w_ada", [DEPTH, D, 6 * D])
    bada_in = din("badaT", [128, DEPTH * 48])
    g_in = din("gT", [128, DEPTH * 4 * 8])
    w_qkv = din("w_qkv", [2, D, 1536])
    qkg_in = din("qkgT", [128, 4])
    w_o = din("w_o", [2, D, D])
    w_pool = din("w_pool", [2, 4, 256, 256])
    psc_in = din("pscT", [128, 16])
    w_up = din("w_up", [DEPTH, D, 2 * DFF])
    w_down = din("w_down", [DEPTH, DFF, D])
    cacheKT = din("cacheKT", [2, 2, 128, PAST])
    cacheV = din("cacheV", [2, PAST, 256])
    rope_in = din("rope", [2, 128, T])
    mask_in = din("maskT", [128, 3 * TB])
    flag_in = din("flag", [128, 1])
    negb_in = din("negb", [128, 1])
    invcnt_in = din("invcnt", [4, 128, T])
    rperm_in = din("rperm", [128, 128])

    yT_out = dout("yT_out", [D, T])
    newkT = dout("newkT", [2, 256, T])
    newv = dout("newv", [2, T, 256])
    if DEBUG:
        dbg_ada = dout("dbg_ada", [DEPTH, 128, 48])
        dbg_y = dout("dbg_y", [128, NCH * TB])
        dbg_hc = dout("dbg_hc", [2, 128, 8 * 272])
        dbg_x = dout("dbg_x", [128, NCH * T])
        dbg_fin = dout("dbg_fin", [2, 128, 8 * 272])
        dbg_tmp2 = dout("dbg_tmp2", [2, 128, T])
        dbg_pooled = dout("dbg_pooled", [2, 128, T])
        dbg_tmpf = dout("dbg_tmpf", [2, 128, T])

    with ExitStack() as st:
        S = Sched(nc, st)

        def sb(name, shape, dt):
            return st.enter_context(nc.sbuf_tensor(name, list(shape), dt))

        xT = sb("xT", [128, NCH, T], F32)
        ones_n = sb("ones_n", [128, 128], BF16)
        ones_h = sb("ones_h", [128, 128], BF16)
        ones_1 = sb("ones_1", [128, 128], BF16)
        rperm = sb("rperm_b", [128, 128], BF16)
        gT = sb("gT_s", [128, DEPTH * 4 * 8], F32)
        qkgT = sb("qkgT_s", [128, 4], F32)
        pscT = sb("pscT_s", [128, 16], F32)
        badaT = sb("badaT_s", [128, DEPTH * 48], F32)
        cond = sb("cond_s", [128, 8], F32)
        scond = sb("scond", [128, 8], F32)
        sc_hi = sb("sc_hi", [128, 8], BF16)
        sc_hf = sb("sc_hf", [128, 8], F32)
        sc_lo = sb("sc_lo", [128, 8], BF16)
        negb = sb("negb_s", [128, 1], F32)
        ones_f = sb("ones_f", [128, 128], F32)
        adaT = [sb("adaT%d" % i, [128, 48], F32) for i in range(2)]
        vecs = [sb("vecs%d" % i, [128, 4, 8], F32) for i in range(2)]
        flag = sb("flag_s", [128, 1], F32)
        dummy = sb("dummy_bar", [128, 8], F32)
        arena_t = sb("arena", [128, ARENA_BYTES // 4], F32)
        AR = Arena(arena_t[:, :])
        psum = st.enter_context(nc.psum_tensor("psum", [128, 8, TB], F32))

        def PS(b):
            return psum[:, b, :], ("ps", b)

        def phase_barrier():
            S.op("dve", lambda e: e.memset(dummy[:], 0.0), writes=["phase"], phase=False)

        S.op("dve", lambda e: e.memset(ones_n[:], 1.0 / D), writes=["ones_n"], phase=False)
        S.op("dve", lambda e: e.memset(ones_h[:], 1.0 / 128), writes=["ones_h"], phase=False)
        S.op("dve", lambda e: e.memset(ones_1[:], 1.0), writes=["ones_1"], phase=False)
        S.op("dve", lambda e: e.memset(ones_f[:], 1.0), writes=["ones_f"], phase=False)
        S.dma("pool", lambda e: e.dma_start(out=rperm[:], in_=rperm_in), writes=["rperm"], phase=False)
        for nm, dst, src in (("gT", gT, g_in), ("qkgT", qkgT, qkg_in), ("pscT", pscT, psc_in),
                             ("badaT", badaT, bada_in), ("cond", cond, cond_in), ("flag", flag, flag_in),
                             ("negb", negb, negb_in)):
            S.dma("sp", lambda e, dst=dst, src=src: e.dma_start(out=dst[:], in_=src), writes=[nm], phase=False)
        xin_v = xT_in.rearrange("(k p) t -> p k t", p=128)
        for k in range(NCH):
            S.dma("sp", lambda e, k=k: e.dma_start(out=xT[:, k, :], in_=xin_v[:, k, :]),
                  writes=[("x", k, tb) for tb in range(NTB)], key=("xin", k), phase=False)
        S.op("act", lambda e: e.activation(out=scond[:], in_=cond[:], func=AF.Silu),
             reads=["cond"], writes=["scond"], phase=False)
        S.op("dve", lambda e: e.tensor_copy(out=sc_hi[:], in_=scond[:]), reads=["scond"], writes=["sc_hi"], phase=False)
        S.op("dve", lambda e: e.tensor_copy(out=sc_hf[:], in_=sc_hi[:]), reads=["sc_hi"], writes=["sc_hf"], phase=False)
        S.op("dve", lambda e: e.tensor_tensor(out=sc_lo[:], in0=scond[:], in1=sc_hf[:], op=ALU.subtract),
             reads=["scond", "sc_hf"], writes=["sc_lo"], phase=False)

        def cols(tb):
            return slice(tb * TB, (tb + 1) * TB)

        class NormTemps:
            def __init__(self):
                self.sq = Rot("sq", [AR.bf16(TB) for _ in range(2)])
                self.r = Rot("r", [AR.f32(TB) for _ in range(1)])
                self.rstd = Rot("rstd", [AR.f32(TB) for _ in range(2)])
                self.tmp = Rot("tmp", [AR.f32(TB) for _ in range(2)])

        def mean_rstd(NT, srcs, src_res, bank, ones, ones_res):
            mps, mres = PS(bank)
            n = len(srcs)
            for i, (s_ap, s_res) in enumerate(zip(srcs, src_res)):
                sq, sqres = NT.sq.next()
                S.op("act", lambda e, sq=sq, s_ap=s_ap: e.activation(out=sq, in_=s_ap, func=AF.Square),
                     reads=[s_res], writes=[sqres])
                S.op("pe", lambda e, sq=sq, i=i: e.matmul(mps, lhsT=ones[:], rhs=sq, start=(i == 0), stop=(i == n - 1)),
                     reads=[sqres, ones_res], writes=[mres])
            r, rres = NT.r.next()
            S.op("act", lambda e: e.activation(out=r, in_=mps, func=AF.Ln, bias=EPS, scale=1.0),
                 reads=[mres], writes=[rres])
            rstd, rsres = NT.rstd.next()
            S.op("act", lambda e: e.activation(out=rstd, in_=r, func=AF.Exp, scale=-0.5), reads=[rres], writes=[rsres])
            return rstd, rsres

        def pre_norm(NT, tb, bank, Av, Bv, vres, outs, out_res):
            c = cols(tb)
            rstd, rsres = mean_rstd(NT, [xT[:, k, c] for k in range(NCH)], [("x", k, tb) for k in range(NCH)],
                                    bank, ones_n, "ones_n")
            for k in range(NCH):
                tmp, tres = NT.tmp.next()
                S.op("dve", lambda e, tmp=tmp, k=k: e.tensor_tensor(out=tmp, in0=xT[:, k, c], in1=rstd, op=ALU.mult),
                     reads=[("x", k, tb), rsres], writes=[tres])
                S.op("dve", lambda e, tmp=tmp, k=k: e.tensor_scalar(out=outs[k], in0=tmp, scalar1=Av[:, k:k + 1],
                                                                    scalar2=Bv[:, k:k + 1], op0=ALU.mult, op1=ALU.add),
                     reads=[tres] + list(vres), writes=[out_res[k]])

        def post_norm(NT, tb, bank, ys, y_res, Gv, vres):
            c = cols(tb)
            rstd, rsres = mean_rstd(NT, ys, y_res, bank, ones_n, "ones_n")
            for k in range(NCH):
                tmp, tres = NT.tmp.next()
                S.op("dve", lambda e, tmp=tmp, k=k: e.tensor_tensor(out=tmp, in0=ys[k], in1=rstd, op=ALU.mult),
                     reads=[y_res[k], rsres], writes=[tres])
                S.op("dve", lambda e, tmp=tmp, k=k: e.scalar_tensor_tensor(
                    out=xT[:, k, c], in0=tmp, scalar=Gv[:, k:k + 1], in1=xT[:, k, c], op0=ALU.mult, op1=ALU.add),
                     reads=[tres, ("x", k, tb)] + list(vres), writes=[("x", k, tb)])

        def norm_groups(NT, tb, bank, srcs, src_res, kind, Av=None, Bv=None, Gv=None, vres=(), outs=None, out_res=None):
            c = cols(tb)
            mps, mres = PS(bank)
            st_ = {}

            def sq_a(k):
                sq, sqres = NT.sq.next()
                st_[("sq", k)] = (sq, sqres)
                S.op("act", lambda e: e.activation(out=sq, in_=srcs[k], func=AF.Square), reads=[src_res[k]], writes=[sqres])

            def sq_b(k):
                sq, sqres = st_.pop(("sq", k))
                S.op("pe", lambda e: e.matmul(mps, lhsT=ones_n[:], rhs=sq, start=(k == 0), stop=(k == NCH - 1)),
                     reads=[sqres, "ones_n"], writes=[mres])

            def rstd_p():
                r, rres = NT.r.next()
                S.op("act", lambda e: e.activation(out=r, in_=mps, func=AF.Ln, bias=EPS, scale=1.0), reads=[mres], writes=[rres])
                rstd, rsres = NT.rstd.next()
                S.op("act", lambda e: e.activation(out=rstd, in_=r, func=AF.Exp, scale=-0.5), reads=[rres], writes=[rsres])
                st_["rstd"] = (rstd, rsres)

            def fin(k):
                rstd, rsres = st_["rstd"]
                tmp, tres = NT.tmp.next()
                S.op("dve", lambda e: e.tensor_tensor(out=tmp, in0=srcs[k], in1=rstd, op=ALU.mult),
                     reads=[src_res[k], rsres], writes=[tres])
                if kind == "pre":
                    S.op("dve", lambda e: e.tensor_scalar(out=outs[k], in0=tmp, scalar1=Av[:, k:k + 1], scalar2=Bv[:, k:k + 1],
                                                          op0=ALU.mult, op1=ALU.add),
                         reads=[tres] + list(vres), writes=[out_res[k]])
                else:
                    S.op("dve", lambda e: e.scalar_tensor_tensor(
                        out=xT[:, k, c], in0=tmp, scalar=Gv[:, k:k + 1], in1=xT[:, k, c], op0=ALU.mult, op1=ALU.add),
                         reads=[tres, ("x", k, tb)] + list(vres), writes=[("x", k, tb)])
            groups = []
            for t in range(NCH // 2 + 1):
                g = []
                for k in (2 * t - 2, 2 * t - 1):
                    if 0 <= k < NCH:
                        g.append(lambda k=k: sq_b(k))
                for k in (2 * t, 2 * t + 1):
                    if 0 <= k < NCH:
                        g.append(lambda k=k: sq_a(k))
                groups.append(g)
            groups.append([rstd_p])
            for t in range(2):
                groups.append([lambda k=k: fin(k) for k in range(4 * t, 4 * t + 4)])
            return groups

        def emit_ada_slice(l, s, WA, adaps, adares):
            wa, wres = WA.next()
            src = w_ada[l].rearrange("(k p) n -> p k n", p=128)[:, :, s * 256:(s + 1) * 256]
            S.dma("pool", lambda e: e.dma_start(out=wa, in_=src), writes=[wres])
            fns = []
            for jj in range(2):
                j = 2 * s + jj
                for k in range(NCH):
                    for hl, sc in enumerate((sc_hi, sc_lo)):
                        fns.append(lambda e, j=j, jj=jj, k=k, hl=hl, sc=sc: e.matmul(
                            adaps[:, j:j + 1], lhsT=wa[:, k, jj * 128:(jj + 1) * 128], rhs=sc[:, k:k + 1],
                            start=(k == 0 and hl == 0), stop=(k == NCH - 1 and hl == 1)))
            S.op("pe", fns, reads=[wres, "sc_hi", "sc_lo"], writes=[adares])

        def finish_ada(l, adaps, adares, part=None):
            a = adaT[l % 2]
            v = vecs[l % 2]
            c0, c1 = {None: (0, 48), "A": (0, 24), "B": (24, 48)}[part]
            S.op("dve", lambda e: e.tensor_tensor(out=a[:, c0:c1], in0=adaps[:, c0:c1],
                                                  in1=badaT[:, l * 48 + c0:l * 48 + c1], op=ALU.add),
                 reads=[adares, "badaT"], writes=[("ada", l % 2)], phase=False)
            if DEBUG and part in (None, "B"):
                S.dma("sp", lambda e: e.dma_start(out=dbg_ada[l], in_=a[:]), reads=[("ada", l % 2)], key=("dbgada", l),
                      phase=False)
            gb = l * 32
            if part in (None, "A"):
                S.op("dve", lambda e: e.scalar_tensor_tensor(out=v[:, 0, :], in0=a[:, 8:16], scalar=1.0,
                                                             in1=gT[:, gb:gb + 8], op0=ALU.add, op1=ALU.mult),
                     reads=[("ada", l % 2), "gT"], writes=[("vec", l % 2, 0)], phase=False)
                S.op("dve", lambda e: e.tensor_tensor(out=v[:, 1, :], in0=a[:, 16:24], in1=gT[:, gb + 8:gb + 16], op=ALU.mult),
                     reads=[("ada", l % 2), "gT"], writes=[("vec", l % 2, 1)], phase=False)
            if part in (None, "B"):
                S.op("dve", lambda e: e.scalar_tensor_tensor(out=v[:, 2, :], in0=a[:, 32:40], scalar=1.0,
                                                             in1=gT[:, gb + 16:gb + 24], op0=ALU.add, op1=ALU.mult),
                     reads=[("ada", l % 2), "gT"], writes=[("vec", l % 2, 2)], phase=False)
                S.op("dve", lambda e: e.tensor_tensor(out=v[:, 3, :], in0=a[:, 40:48], in1=gT[:, gb + 24:gb + 32], op=ALU.mult),
                     reads=[("ada", l % 2), "gT"], writes=[("vec", l % 2, 3)], phase=False)

        phase_barrier()
        AR.reset()
        WA0 = Rot("wa", [AR.bf16(8 * 256).rearrange("p (k n) -> p k n", k=8) for _ in range(3)])
        adaps0, adares0 = PS(7)
        for s in range(12):
            emit_ada_slice(0, s, WA0, adaps0, adares0)
        finish_ada(0, adaps0, adares0, part="A")

        def layer_body(l):
            a_T = adaT[l % 2]
            v_T = vecs[l % 2]
            A1, G1, A2, G2 = (v_T[:, i, :] for i in range(4))
            B1 = a_T[:, 0:8]
            B2 = a_T[:, 24:32]
            vr = [("ada", l % 2)] + [("vec", l % 2, i) for i in range(4)]

            if l % 2 == 0:
                a = l // 2
                phase_barrier()
                AR.reset()
                hT = AR.bf16(NCH * T).rearrange("p (k t) -> p k t", k=NCH)
                QT = AR.bf16(NCH * T).rearrange("p (k t) -> p k t", k=NCH)
                KT = AR.bf16(2 * (T + PAST)).rearrange("p (k t) -> p k t", k=2)
                Vb = AR.bf16(NKT * 256).rearrange("p (k c) -> p k c", k=NKT)
                mark = AR.off
                WQ = Rot("wq", [AR.bf16(NCH * 128).rearrange("p (k n) -> p k n", k=NCH) for _ in range(3)])
                NT = NormTemps()
                qn_r = Rot("qn", [AR.f32(TB) for _ in range(2)])
                qnb_r = Rot("qnb", [AR.bf16(TB) for _ in range(2)])
                t1_r = Rot("t1", [AR.f32(TB) for _ in range(2)])
                t2_r = Rot("t2", [AR.f32(TB) for _ in range(2)])
                rope_r = Rot("ropeb", [AR.f32(2 * TB).rearrange("p (a n) -> p a n", a=2) for _ in range(2)])
                vst_r = Rot("vst", [AR.f32(128) for _ in range(2)])

                for kvh in range(2):
                    S.dma("pool", lambda e, kvh=kvh: e.dma_start(out=KT[:, kvh, T:T + PAST], in_=cacheKT[a, kvh]),
                          writes=[("KTc", kvh)])
                S.dma("pool", lambda e: e.dma_start(out=Vb[:, 16:20, :],
                                                    in_=cacheV[a].rearrange("(i p) c -> p i c", p=128)),
                      writes=["Vc"])

                wq_v = w_qkv[a].rearrange("(k p) n -> p k n", p=128)
                wq_of = {}

                def load_wq(j):
                    wq, wqres = WQ.next()
                    c0 = j * 128 if j < 10 else 1280 + (j - 10) * 128
                    S.dma("pool", lambda e: e.dma_start(out=wq, in_=wq_v[:, :, c0:c0 + 128]), writes=[wqres])
                    wq_of[j] = (wq, wqres)

                load_wq(0)
                load_wq(1)
                for tb in range(NTB):
                    pre_norm(NT, tb, 3, A1, B1, vr, [hT[:, k, cols(tb)] for k in range(NCH)],
                             [("hT", k, tb) for k in range(NCH)])

                QK = [(j, tb) for j in range(10) for tb in range(NTB)]
                stt_ = {}
                qn3_r = Rot("qn", [qn_r.aps[0], qn_r.aps[1], AR.f32(TB)])

                def stA(i):
                    j, tb = QK[i]
                    if tb == 0 and j + 2 < 12:
                        load_wq(j + 2)
                    wq, wqres = wq_of[j]
                    c = cols(tb)
                    qps, qres = PS(i % 3)
                    S.op("pe", [lambda e, k=k: e.matmul(qps, lhsT=wq[:, k, :], rhs=hT[:, k, c],
                                                        start=(k == 0), stop=(k == NCH - 1)) for k in range(NCH)],
                         reads=[wqres] + [("hT", k, tb) for k in range(NCH)], writes=[qres])
                    stt_[i] = dict(qps=qps, qres=qres)

                def stB1(i):
                    d = stt_[i]
                    qps, qres = d["qps"], d["qres"]
                    sq, sqres = NT.sq.next()
                    mps, mres = PS(3 if i % 2 == 0 else 6)
                    S.op("act", lambda e: e.activation(out=sq, in_=qps, func=AF.Square), reads=[qres], writes=[sqres])
                    S.op("pe", lambda e: e.matmul(mps, lhsT=ones_h[:], rhs=sq, start=True, stop=True),
                         reads=[sqres, "ones_h"], writes=[mres])
                    d.update(mps=mps, mres=mres)

                def stB2(i):
                    j, tb = QK[i]
                    d = stt_[i]
                    qps, qres, mps, mres = d["qps"], d["qres"], d["mps"], d["mres"]
                    gcol = a * 2 + (0 if j < 8 else 1)
                    r, rres = NT.r.next()
                    S.op("act", lambda e: e.activation(out=r, in_=mps, func=AF.Ln, bias=EPS, scale=1.0),
                         reads=[mres], writes=[rres])
                    rstd, rsres = NT.rstd.next()
                    S.op("act", lambda e: e.activation(out=rstd, in_=r, func=AF.Exp, scale=-0.5), reads=[rres], writes=[rsres])
                    qn, qnres = qn3_r.next()
                    S.op("dve", lambda e: e.scalar_tensor_tensor(
                        out=qn, in0=qps, scalar=qkgT[:, gcol:gcol + 1], in1=rstd, op0=ALU.mult, op1=ALU.mult),
                         reads=[qres, rsres, "qkgT"], writes=[qnres])
                    d.update(qn=qn, qnres=qnres)

                def stB3(i):
                    j, tb = QK[i]
                    d = stt_[i]
                    c = cols(tb)
                    qn, qnres = d["qn"], d["qnres"]
                    if j >= 8:
                        S.dma("sp", lambda e: e.dma_start(out=newkT[a, (j - 8) * 128:(j - 7) * 128, c], in_=qn),
                              reads=[qnres])
                    qnb, qnbres = qnb_r.next()
                    S.op("act", lambda e: e.activation(out=qnb, in_=qn, func=AF.Copy), reads=[qnres], writes=[qnbres])
                    rp, rpres = rope_r.next()
                    S.dma("sp", lambda e: e.dma_start(out=rp, in_=rope_in[:, :, c].rearrange("a p n -> p a n")),
                          writes=[rpres])
                    d.update(qnb=qnb, qnbres=qnbres, rp=rp, rpres=rpres)

                def stC(i):
                    j, tb = QK[i]
                    d = stt_.pop(i)
                    c = cols(tb)
                    qn, qnres, qnb, qnbres, rp, rpres = d["qn"], d["qnres"], d["qnb"], d["qnbres"], d["rp"], d["rpres"]
                    rps, rres = PS(4 + i % 2)
                    S.op("pe", lambda e: e.matmul(rps, lhsT=rperm[:], rhs=qnb, start=True, stop=True),
                         reads=[qnbres, "rperm"], writes=[rres])
                    t1, t1res = t1_r.next()
                    t2, t2res = t2_r.next()
                    S.op("dve", lambda e: e.tensor_tensor(out=t1, in0=qn, in1=rp[:, 0, :], op=ALU.mult),
                         reads=[qnres, rpres], writes=[t1res])
                    S.op("dve", lambda e: e.tensor_tensor(out=t2, in0=rps, in1=rp[:, 1, :], op=ALU.mult),
                         reads=[rres, rpres], writes=[t2res])
                    if j < 8:
                        dst, dres = QT[:, j, c], ("QT", j, tb)
                    else:
                        dst, dres = KT[:, j - 8, c], ("KT", j - 8, tb)
                    S.op("dve", lambda e: e.tensor_tensor(out=dst, in0=t1, in1=t2, op=ALU.add),
                         reads=[t1res, t2res], writes=[dres])

                nq = len(QK)
                stages = (stA, stB1, stB2, stB3, stC)
                ada_q = 12 if l == 0 else 24
                if l == 0:
                    WAq = Rot("wa", [AR.bf16(8 * 256).rearrange("p (k n) -> p k n", k=8) for _ in range(2)])
                    adapsq, adaresq = PS(7)
                for it in range(nq + len(stages) - 1):
                    for si, fn in enumerate(stages):
                        if 0 <= it - si < nq:
                            fn(it - si)
                    if ada_q < 24 and it >= 2 and it % 3 == 2:
                        emit_ada_slice(0, ada_q, WAq, adapsq, adaresq)
                        ada_q += 1
                if l == 0:
                    while ada_q < 24:
                        emit_ada_slice(0, ada_q, WAq, adapsq, adaresq)
                        ada_q += 1
                    finish_ada(0, adapsq, adaresq, part="B")
                vi = 0
                for vs in range(2):
                    wq, wqres = wq_of[10 + vs]
                    for kt in range(16):
                        vps, vres_ = PS(3 if vi % 2 == 0 else 6)
                        vi += 1
                        tbk = kt // 4
                        S.op("pe", [lambda e, k=k, wq=wq, vps=vps, kt=kt: e.matmul(
                            vps[:, 0:128], lhsT=hT[:, k, kt * 128:(kt + 1) * 128], rhs=wq[:, k, :],
                            start=(k == 0), stop=(k == NCH - 1)) for k in range(NCH)],
                             reads=[wqres] + [("hT", k, tbk) for k in range(NCH)], writes=[vres_])
                        vst, vstres = vst_r.next()
                        S.op("act", lambda e, vst=vst, vps=vps: e.activation(out=vst, in_=vps[:, 0:128], func=AF.Copy),
                             reads=[vres_], writes=[vstres])
                        S.dma("sp", lambda e, vst=vst, kt=kt, vs=vs: e.dma_start(
                            out=newv[a, kt * 128:(kt + 1) * 128, vs * 128:(vs + 1) * 128], in_=vst), reads=[vstres])
                        S.op("pool", lambda e, vst=vst, kt=kt, vs=vs: e.tensor_copy(
                            out=Vb[:, kt, vs * 128:(vs + 1) * 128], in_=vst), reads=[vstres], writes=[("Vb", kt, vs)])

                phase_barrier()
                AR.reset()
                WO = AR.bf16(NCH * D).rearrange("p (k n) -> p k n", k=NCH)
                yT = AR.f32(NCH * TB).rearrange("p (k t) -> p k t", k=NCH)
                assert AR.off <= 32768
                AR.off = mark
                wo_v = w_o[a].rearrange("(k p) n -> p k n", p=128)
                for k in range(NCH):
                    S.dma("pool", lambda e, k=k: e.dma_start(out=WO[:, k, :], in_=wo_v[:, k, :]), writes=[("WO", k)])
                PT_r = Rot("PT", [AR.bf16(2 * TB).rearrange("p (a n) -> p a n", a=2) for _ in range(6)])
                tp_r = Rot("tpair", [AR.bf16(TB) for _ in range(6)])
                acc_r = Rot("acc", [AR.f32(TB) for _ in range(2)])
                accb_r = Rot("accb", [AR.bf16(TB) for _ in range(2)])
                rec_r = Rot("rec", [AR.f32(TB) for _ in range(2)])
                lnr_r = Rot("lnr", [AR.f32(TB) for _ in range(1)])
                NT = NormTemps()
                scale = 1.0 / float(np.sqrt(128.0))
                NKP = NKT // 2
                pairs = [(qb, h, kp) for qb in range(NTB) for h in range(NCH) for kp in range(NKP)]
                pend = {}
                accs = {}
                hold = {}

                def emit_S(i):
                    qb, h, kp = pairs[i]
                    kvh = h // 4
                    slot = i % 2
                    sp2 = psum[:, 2 * slot:2 * slot + 2, :]
                    sres = [("ps", 2 * slot), ("ps", 2 * slot + 1)]
                    kres = [("KT", kvh, kp // 2)] if kp < 8 else [("KTc", kvh)]
                    S.op("pe", [lambda e, u=u: e.matmul(psum[:, 2 * slot + u, :],
                                                        lhsT=KT[:, kvh, (2 * kp + u) * 128:(2 * kp + u + 1) * 128],
                                                        rhs=QT[:, h, cols(qb)], start=True, stop=True) for u in range(2)],
                         reads=kres + [("QT", h, qb)], writes=sres)
                    pt, ptres = PT_r.next()
                    own = (kp - 2 * qb) if kp < 8 else -1
                    if own in (0, 1):
                        fns = []
                        for half in range(2):
                            hsl = slice(half * 256, (half + 1) * 256)
                            if half == own:
                                fns.append(lambda e, hsl=hsl: e.activation(out=pt[:, :, hsl], in_=sp2[:, :, hsl], func=AF.Exp,
                                                                           scale=scale))
                            else:
                                fns.append(lambda e, hsl=hsl: e.activation(out=pt[:, :, hsl], in_=sp2[:, :, hsl], func=AF.Exp,
                                                                           bias=negb[:, 0:1], scale=scale))
                        S.op("act", fns, reads=sres + ["negb"], writes=[ptres])
                    else:
                        S.op("act", lambda e: e.activation(out=pt, in_=sp2, func=AF.Exp, bias=negb[:, 0:1], scale=scale),
                             reads=sres + ["negb"], writes=[ptres])
                    tpb, tpres = tp_r.next()
                    S.op("dve", lambda e: e.tensor_tensor(out=tpb, in0=pt[:, 0, :], in1=pt[:, 1, :], op=ALU.add),
                         reads=[ptres], writes=[tpres])
                    if kp == 0:
                        accs[(qb, h)] = acc_r.next()
                    acc, accres = accs[(qb, h)]
                    if kp % 2 == 0:
                        hold[(qb, h)] = (tpb, tpres)
                    else:
                        tpa, tpares = hold.pop((qb, h))
                        S.op("dve", lambda e: e.tensor_tensor(out=tpb, in0=tpa, in1=tpb, op=ALU.add),
                             reads=[tpares, tpres], writes=[tpres])
                        if kp == 1:
                            S.op("dve", lambda e: e.tensor_copy(out=acc, in_=tpb), reads=[tpres], writes=[accres])
                        else:
                            S.op("dve", lambda e: e.tensor_tensor(out=acc, in0=acc, in1=tpb, op=ALU.add),
                                 reads=[tpres, accres], writes=[accres])
                    pend[i] = (pt, ptres)

                def emit_PV(i):
                    qb, h, kp = pairs[i]
                    kvh = h // 4
                    g = (qb * NCH + h) % 2
                    ops_, ores = PS(4 + g)
                    pt, ptres = pend.pop(i)
                    vres_ = [("Vb", 2 * kp, kvh), ("Vb", 2 * kp + 1, kvh)] if kp < 8 else ["Vc"]
                    S.op("pe", [lambda e, u=u: e.matmul(ops_, lhsT=Vb[:, 2 * kp + u, kvh * 128:(kvh + 1) * 128], rhs=pt[:, u, :],
                                                        start=(kp == 0 and u == 0), stop=(kp == NKP - 1 and u == 1))
                                for u in range(2)],
                         reads=[ptres] + vres_, writes=[ores])
                    if kp == NKP - 1:
                        acc, accres = accs.pop((qb, h))
                        sums, sures = PS(6)
                        lnr, lnres = lnr_r.next()
                        rec, recres = rec_r.next()

                        def t_sum():
                            S.op("pe", lambda e: e.matmul(sums, lhsT=ones_f[:], rhs=acc, start=True, stop=True),
                                 reads=[accres, "ones_f"], writes=[sures])

                        def t_rec():
                            S.op("act", lambda e: e.activation(out=lnr, in_=sums, func=AF.Ln), reads=[sures], writes=[lnres])
                            S.op("act", lambda e: e.activation(out=rec, in_=lnr, func=AF.Exp, scale=-1.0),
                                 reads=[lnres], writes=[recres])

                        def t_fin():
                            S.op("dve", lambda e: e.tensor_tensor(out=QT[:, h, cols(qb)], in0=ops_, in1=rec, op=ALU.mult),
                                 reads=[ores, recres, ("QT", h, qb)], writes=[("AT", h, qb)])
                        sched_at(i + 2, t_sum)
                        sched_at(i + 3, t_rec)
                        sched_at(i + 5, t_fin)

                def sched_wo(qb, base):
                    c = cols(qb)
                    ys = [yT[:, m, :] for m in range(NCH)]
                    yres_l = [("yT", m) for m in range(NCH)]
                    mps, mres = PS(7)
                    st_ = {}

                    def wo_mm(m):
                        yps, yres = PS(7)
                        S.op("pe", [lambda e, hh=hh: e.matmul(yps, lhsT=WO[:, hh, m * 128:(m + 1) * 128],
                                                              rhs=QT[:, hh, c], start=(hh == 0), stop=(hh == NCH - 1))
                                    for hh in range(NCH)],
                             reads=[("WO", hh) for hh in range(NCH)] + [("AT", hh, qb) for hh in range(NCH)], writes=[yres])

                    def wo_cp(m):
                        yps, yres = PS(7)
                        S.op("dve", lambda e: e.tensor_copy(out=yT[:, m, :], in_=yps), reads=[yres], writes=[("yT", m)])

                    def sq_a(k):
                        sq, sqres = NT.sq.next()
                        st_[("sq", k)] = (sq, sqres)
                        S.op("act", lambda e: e.activation(out=sq, in_=ys[k], func=AF.Square), reads=[yres_l[k]], writes=[sqres])

                    def sq_b(k):
                        sq, sqres = st_.pop(("sq", k))
                        S.op("pe", lambda e: e.matmul(mps, lhsT=ones_n[:], rhs=sq, start=(k == 0), stop=(k == NCH - 1)),
                             reads=[sqres, "ones_n"], writes=[mres])

                    def rstd_piece():
                        r, rres = NT.r.next()
                        S.op("act", lambda e: e.activation(out=r, in_=mps, func=AF.Ln, bias=EPS, scale=1.0),
                             reads=[mres], writes=[rres])
                        rstd, rsres = NT.rstd.next()
                        S.op("act", lambda e: e.activation(out=rstd, in_=r, func=AF.Exp, scale=-0.5), reads=[rres], writes=[rsres])
                        st_["rstd"] = (rstd, rsres)

                    def upd_piece(k):
                        rstd, rsres = st_["rstd"]
                        tmp, tres = NT.tmp.next()
                        S.op("dve", lambda e: e.tensor_tensor(out=tmp, in0=ys[k], in1=rstd, op=ALU.mult),
                             reads=[yres_l[k], rsres], writes=[tres])
                        S.op("dve", lambda e: e.scalar_tensor_tensor(
                            out=xT[:, k, c], in0=tmp, scalar=G1[:, k:k + 1], in1=xT[:, k, c], op0=ALU.mult, op1=ALU.add),
                             reads=[tres, ("x", k, qb)] + list(vr), writes=[("x", k, qb)])
                    for m in range(NCH):
                        sched_at(base + 2 * m, lambda m=m: wo_mm(m))
                        sched_at(base + 2 * m + 1, lambda m=m: wo_cp(m))
                    b2 = base + 2 * NCH + 2
                    for k in range(NCH):
                        sched_at(b2 + 2 * k, lambda k=k: sq_a(k))
                        sched_at(b2 + 2 * k + 1, lambda k=k: sq_b(k))
                    b3 = b2 + 2 * NCH + 1
                    sched_at(b3, rstd_piece)
                    for k in range(NCH):
                        sched_at(b3 + 2 + k, lambda k=k: upd_piece(k))

                n = len(pairs)
                per_qb = NCH * NKP
                due = {}
                seq_ = [0]

                def sched_at(it, fn):
                    due.setdefault(it, []).append(fn)

                emit_S(0)
                emit_S(1)
                for i in range(n):
                    if i + 2 < n:
                        emit_S(i + 2)
                    emit_PV(i)
                    if (i + 1) % per_qb == 0:
                        sched_wo((i + 1) // per_qb - 1, i + 8)
                    for fn in due.pop(i, []):
                        fn()
                for it in sorted(due):
                    for fn in due[it]:
                        fn()
            else:
                p = l // 2
                phase_barrier()
                AR.reset()
                hc_r = Rot("hc", [AR.f32(8 * 272).rearrange("p (s c) -> p s c", s=8) for _ in range(2)])
                s1 = AR.f32(8 * 272).rearrange("p (s c) -> p s c", s=8)
                s2 = AR.f32(8 * 272).rearrange("p (s c) -> p s c", s=8)
                pooledT = AR.bf16(NCH * T).rearrange("p (k t) -> p k t", k=NCH)
                yT2 = [AR.f32(NCH * TB).rearrange("p (k t) -> p k t", k=NCH) for _ in range(2)]
                WP = AR.bf16(4 * 2 * 256).rearrange("p (g k n) -> p g k n", g=4, k=2)
                rstdb = AR.f32(T)
                invc_r = Rot("invc", [AR.f32(T) for _ in range(1)])
                tmpf = AR.f32(T)
                tmpe = AR.f32(64).rearrange("p (s c) -> p s c", s=8)
                NT = NormTemps()
                for g in range(4):
                    S.dma("pool", lambda e, g=g: e.dma_start(out=WP[:, g, :, :],
                                                             in_=w_pool[p, g].rearrange("(k p) n -> p k n", p=128)),
                          writes=[("WP", g)])
                for i in range(2):
                    hcb = hc_r.aps[i]
                    S.op("dve", lambda e, hcb=hcb: e.memset(hcb, 0.0), writes=[("hc", i)])
                for tb in range(NTB):
                    c = cols(tb)
                    rstd, rsres = mean_rstd(NT, [xT[:, k, c] for k in range(NCH)], [("x", k, tb) for k in range(NCH)],
                                            tb % 2, ones_n, "ones_n")
                    S.op("dve", lambda e, c=c, rstd=rstd: e.tensor_copy(out=rstdb[:, c], in_=rstd),
                         reads=[rsres], writes=[("rstdb", tb)])
                invc = None
                for k in range(NCH):
                    g = k // 2
                    w = POOL_W[g]
                    if k % 2 == 0:
                        invc, invres = invc_r.next()
                        S.dma("sp", lambda e, invc=invc, g=g: e.dma_start(out=invc, in_=invcnt_in[g]), writes=[invres])
                    hc, hres = hc_r.next()
                    S.op("dve", lambda e, k=k: e.tensor_tensor(out=tmpf, in0=xT[:, k, :], in1=rstdb, op=ALU.mult),
                         reads=[("x", k, tb) for tb in range(NTB)] + [("rstdb", tb) for tb in range(NTB)], writes=["tmpf"])
                    S.op("dve", lambda e, hc=hc, k=k: e.tensor_scalar(
                        out=hc[:, :, 8:264], in0=tmpf.rearrange("p (s c) -> p s c", s=8), scalar1=A1[:, k:k + 1],
                        scalar2=B1[:, k:k + 1], op0=ALU.mult, op1=ALU.add), reads=["tmpf"] + vr, writes=[hres])
                    S.op("dve", lambda e, hc=hc: e.tensor_scalar(out=hc[:, 1:8, 0:8], in0=hc[:, 0:7, 256:264],
                                                                 scalar1=flag[:, 0:1], scalar2=None, op0=ALU.mult),
                         reads=[hres, "flag"], writes=[hres])
                    S.op("dve", lambda e, hc=hc: e.tensor_scalar(out=hc[:, 0:7, 264:272], in0=hc[:, 1:8, 8:16],
                                                                 scalar1=flag[:, 0:1], scalar2=None, op0=ALU.mult),
                         reads=[hres, "flag"], writes=[hres])
                    S.op("dve", lambda e, hc=hc: e.tensor_tensor(out=s1[:, :, 1:272], in0=hc[:, :, 0:271], in1=hc[:, :, 1:272],
                                                                 op=ALU.add), reads=[hres], writes=["s1"])
                    fin, finres = s1, "s1"
                    if w >= 4:
                        S.op("dve", lambda e: e.tensor_tensor(out=s2[:, :, 2:271], in0=s1[:, :, 1:270], in1=s1[:, :, 3:272],
                                                              op=ALU.add), reads=["s1"], writes=["s2"])
                        fin, finres = s2, "s2"
                    if w >= 8:
                        S.op("dve", lambda e: e.tensor_tensor(out=s1[:, :, 4:269], in0=s2[:, :, 2:267], in1=s2[:, :, 6:271],
                                                              op=ALU.add), reads=["s2"], writes=["s1"])
                        fin, finres = s1, "s1"
                    if w >= 16:
                        S.op("dve", lambda e: e.tensor_tensor(out=s2[:, :, 8:264], in0=s1[:, :, 4:260], in1=s1[:, :, 12:268],
                                                              op=ALU.add), reads=["s1"], writes=["s2"])
                        fin, finres = s2, "s2"
                    pv = pooledT[:, k, :].rearrange("p (s c) -> p s c", s=8)
                    iv = invc.rearrange("p (s c) -> p s c", s=8)
                    S.op("dve", lambda e, fin=fin, hc=hc, pv=pv, w=w: e.scalar_tensor_tensor(
                        out=pv, in0=fin[:, :, 8:264], scalar=1.0 / w, in1=hc[:, :, 8:264], op0=ALU.mult, op1=ALU.subtract),
                         reads=[finres, hres], writes=[("pooled", k)])
                    for (c0, c1) in ((0, 8), (248, 256)):
                        S.op("dve", lambda e, fin=fin, iv=iv, c0=c0, c1=c1: e.tensor_tensor(
                            out=tmpe, in0=fin[:, :, 8 + c0:8 + c1], in1=iv[:, :, c0:c1], op=ALU.mult),
                             reads=[finres, invres], writes=["tmpe"])
                        S.op("dve", lambda e, hc=hc, pv=pv, c0=c0, c1=c1: e.tensor_tensor(
                            out=pv[:, :, c0:c1], in0=tmpe, in1=hc[:, :, 8 + c0:8 + c1], op=ALU.subtract),
                             reads=["tmpe", hres, ("pooled", k)], writes=[("pooled", k)])
                    if DEBUG and l == 1 and k in (0, 7):
                        di = 0 if k == 0 else 1
                        S.dma("sp", lambda e, hc=hc, di=di: e.dma_start(out=dbg_hc[di], in_=hc.rearrange("p s c -> p (s c)")),
                              reads=[hres], key=("dbg", "hc", di))
                        S.dma("sp", lambda e, fin=fin, di=di: e.dma_start(out=dbg_fin[di], in_=fin.rearrange("p s c -> p (s c)")),
                              reads=[finres], key=("dbg", "fin", di))
                        S.dma("sp", lambda e, di=di: e.dma_start(out=dbg_tmpf[di], in_=tmpf), reads=["tmpf"], key=("dbg", "tmpf", di))
                        S.dma("pool", lambda e, di=di, k=k: e.dma_start(out=dbg_pooled[di], in_=pooledT[:, k, :]),
                              reads=[("pooled", k)], key=("dbg", "pooled", di))
                yi = 0

                def pool_mm(tb):
                    nonlocal yi
                    c = cols(tb)
                    yT = yT2[tb % 2]
                    for m in range(NCH):
                        g, oc = m // 2, m % 2
                        yps, yres = PS(2 + yi % 2)
                        yi += 1
                        S.op("pe", [lambda e, kc=kc, g=g, oc=oc, yps=yps, c=c: e.matmul(
                            yps, lhsT=WP[:, g, kc, oc * 128:(oc + 1) * 128], rhs=pooledT[:, 2 * g + kc, c],
                            start=(kc == 0), stop=(kc == 1)) for kc in range(2)],
                             reads=[("WP", g), ("pooled", 2 * g), ("pooled", 2 * g + 1)], writes=[yres])
                        S.op("act", lambda e, m=m, yps=yps, yT=yT: e.activation(out=yT[:, m, :], in_=yps, func=AF.Identity,
                                                                                scale=pscT[:, p * 8 + m:p * 8 + m + 1]),
                             reads=[yres, "pscT"], writes=[("yT", tb % 2, m)])
                        if pside:
                            for fn in pside.pop(0):
                                fn()

                def pool_post(tb):
                    yT = yT2[tb % 2]
                    post_norm(NT, tb, 4, [yT[:, m, :] for m in range(NCH)], [("yT", tb % 2, m) for m in range(NCH)], G1, vr)

                pside = []
                pool_mm(0)
                for tb in range(NTB):
                    yTb = yT2[tb % 2]
                    pside.extend(norm_groups(NT, tb, 4, [yTb[:, m, :] for m in range(NCH)],
                                             [("yT", tb % 2, m) for m in range(NCH)], "post", Gv=G1, vres=vr))
                    if tb + 1 < NTB:
                        pool_mm(tb + 1)
                    while pside:
                        for fn in pside.pop(0):
                            fn()

            phase_barrier()
            AR.reset()
            HB = 2 * TB
            hTh = AR.bf16(NCH * HB).rearrange("p (k t) -> p k t", k=NCH)
            yTh = AR.f32(NCH * HB).rearrange("p (k t) -> p k t", k=NCH)
            hmid = AR.bf16(NF * HB).rearrange("p (j t) -> p j t", j=NF)
            WU = Rot("wu", [AR.bf16(NCH * 256).rearrange("p (k n) -> p k n", k=NCH) for _ in range(3)])
            WD = Rot("wd", [AR.bf16(NF * 128).rearrange("p (j n) -> p j n", j=NF) for _ in range(2)])
            WA = Rot("wa", [AR.bf16(8 * 256).rearrange("p (k n) -> p k n", k=8) for _ in range(2)])
            sg_r = Rot("sg", [AR.f32(TB) for _ in range(2)])
            NT = NormTemps()
            wup_v = w_up[l].rearrange("(k p) n -> p k n", p=128)
            wdn_v = w_down[l].rearrange("(j p) n -> p j n", p=128)
            do_ada = (l + 1 < n_layers)
            adaps, adares = PS(7)
            ada_s = 0
            ui = 0
            side = []

            def side_step():
                if side:
                    for fn in side.pop(0):
                        fn()

            def side_flush():
                while side:
                    side_step()

            def pre_groups(hf):
                gs = []
                for tt in range(2):
                    tb = 2 * hf + tt
                    gs += norm_groups(NT, tb, 6, [xT[:, k, cols(tb)] for k in range(NCH)], [("x", k, tb) for k in range(NCH)],
                                      "pre", Av=A2, Bv=B2, vres=vr,
                                      outs=[hTh[:, k, tt * TB:(tt + 1) * TB] for k in range(NCH)],
                                      out_res=[("hTh", k, tt) for k in range(NCH)])
                return gs

            def post_groups(hf):
                gs = []
                for tt in range(2):
                    hs = slice(tt * TB, (tt + 1) * TB)
                    gs += norm_groups(NT, 2 * hf + tt, 6, [yTh[:, m, hs] for m in range(NCH)],
                                      [("yTh", m, tt) for m in range(NCH)], "post", Gv=G2, vres=vr)
                return gs

            def ffn_pre(hf):
                nonlocal ui, ada_s
                for tt in range(2):
                    tb = 2 * hf + tt
                    pre_norm(NT, tb, 6, A2, B2, vr, [hTh[:, k, tt * TB:(tt + 1) * TB] for k in range(NCH)],
                             [("hTh", k, tt) for k in range(NCH)])
            def ffn_up(hf):
                nonlocal ui, ada_s
                for j in range(NF):
                    wu, wures = WU.next()
                    S.dma("pool", lambda e, wu=wu, j=j: e.dma_start(out=wu[:, :, 0:128], in_=wup_v[:, :, j * 128:(j + 1) * 128]),
                          writes=[wures], key=("wu", WU.i, 0))
                    S.dma("pool", lambda e, wu=wu, j=j: e.dma_start(out=wu[:, :, 128:256],
                                                                     in_=wup_v[:, :, DFF + j * 128:DFF + (j + 1) * 128]),
                          writes=[(wures, "u")], key=("wu", WU.i, 1))
                    for tt in range(2):
                        gps, gres = PS(ui % 2)
                        ups, ures = PS(2 + ui % 2)
                        ui += 1
                        hs = slice(tt * TB, (tt + 1) * TB)
                        S.op("pe", [lambda e, k=k, wu=wu, gps=gps, hs=hs: e.matmul(
                            gps, lhsT=wu[:, k, 0:128], rhs=hTh[:, k, hs], start=(k == 0), stop=(k == NCH - 1))
                            for k in range(NCH)],
                             reads=[wures] + [("hTh", k, tt) for k in range(NCH)], writes=[gres])
                        S.op("pe", [lambda e, k=k, wu=wu, ups=ups, hs=hs: e.matmul(
                            ups, lhsT=wu[:, k, 128:256], rhs=hTh[:, k, hs], start=(k == 0), stop=(k == NCH - 1))
                            for k in range(NCH)],
                             reads=[(wures, "u")] + [("hTh", k, tt) for k in range(NCH)], writes=[ures])
                        sg, sgres = sg_r.next()
                        S.op("act", lambda e, sg=sg, gps=gps: e.activation(out=sg, in_=gps, func=AF.Silu),
                             reads=[gres], writes=[sgres])
                        S.op("dve", lambda e, sg=sg, ups=ups, j=j, hs=hs: e.tensor_tensor(
                            out=hmid[:, j, hs], in0=sg, in1=ups, op=ALU.mult),
                             reads=[sgres, ures], writes=[("hmid", j, tt)])
                        if j >= 1:
                            side_step()
                    if do_ada and ada_s < 24:
                        emit_ada_slice(l + 1, ada_s, WA, adaps, adares)
                        ada_s += 1
            def ffn_down(hf):
                nonlocal ui, ada_s
                for m in range(NCH):
                    wd, wdres = WD.next()
                    S.dma("pool", lambda e, wd=wd, m=m: e.dma_start(out=wd, in_=wdn_v[:, :, m * 128:(m + 1) * 128]),
                          writes=[wdres])
                    for tt in range(2):
                        yps, yres = PS(4 + (2 * m + tt) % 2)
                        hs = slice(tt * TB, (tt + 1) * TB)
                        S.op("pe", [lambda e, j=j, wd=wd, yps=yps, hs=hs: e.matmul(
                            yps, lhsT=wd[:, j, :], rhs=hmid[:, j, hs], start=(j == 0), stop=(j == NF - 1))
                            for j in range(NF)],
                             reads=[wdres] + [("hmid", j, tt) for j in range(NF)], writes=[yres])
                        S.op("act", lambda e, m=m, yps=yps, hs=hs: e.activation(out=yTh[:, m, hs], in_=yps, func=AF.Copy),
                             reads=[yres], writes=[("yTh", m, tt)])
                        side_step()
            def ffn_post(hf):
                nonlocal ui, ada_s
                for tt in range(2):
                    hs = slice(tt * TB, (tt + 1) * TB)
                    post_norm(NT, 2 * hf + tt, 6, [yTh[:, m, hs] for m in range(NCH)],
                              [("yTh", m, tt) for m in range(NCH)], G2, vr)
            def ffn_down_last(hf):
                st_ = {}
                mp = {0: PS(6), 1: PS(0)}

                def sq_a(m, tt):
                    hs = slice(tt * TB, (tt + 1) * TB)
                    sq, sqres = NT.sq.next()
                    st_[(m, tt)] = (sq, sqres)
                    S.op("act", lambda e: e.activation(out=sq, in_=yTh[:, m, hs], func=AF.Square),
                         reads=[("yTh", m, tt)], writes=[sqres])

                def sq_b(m, tt):
                    sq, sqres = st_.pop((m, tt))
                    mps, mres = mp[tt]
                    S.op("pe", lambda e: e.matmul(mps, lhsT=ones_n[:], rhs=sq, start=(m == 0), stop=(m == NCH - 1)),
                         reads=[sqres, "ones_n"], writes=[mres])

                for m in range(NCH):
                    wd, wdres = WD.next()
                    S.dma("pool", lambda e, wd=wd, m=m: e.dma_start(out=wd, in_=wdn_v[:, :, m * 128:(m + 1) * 128]),
                          writes=[wdres])
                    for tt in range(2):
                        yps, yres = PS(4 + (2 * m + tt) % 2)
                        hs = slice(tt * TB, (tt + 1) * TB)
                        S.op("pe", [lambda e, j=j, wd=wd, yps=yps, hs=hs: e.matmul(
                            yps, lhsT=wd[:, j, :], rhs=hmid[:, j, hs], start=(j == 0), stop=(j == NF - 1))
                            for j in range(NF)],
                             reads=[wdres] + [("hmid", j, tt) for j in range(NF)], writes=[yres])
                        S.op("act", lambda e, m=m, yps=yps, hs=hs: e.activation(out=yTh[:, m, hs], in_=yps, func=AF.Copy),
                             reads=[yres], writes=[("yTh", m, tt)])
                    if m >= 1:
                        sq_b(m - 1, 0)
                        sq_b(m - 1, 1)
                    sq_a(m, 0)
                    sq_a(m, 1)
                sq_b(NCH - 1, 0)
                sq_b(NCH - 1, 1)
                rs = {}
                for tt in range(2):
                    mps, mres = mp[tt]
                    r, rres = NT.r.next()
                    S.op("act", lambda e, r=r, mps=mps: e.activation(out=r, in_=mps, func=AF.Ln, bias=EPS, scale=1.0),
                         reads=[mres], writes=[rres])
                    rstd, rsres = NT.rstd.next()
                    S.op("act", lambda e, r=r, rstd=rstd: e.activation(out=rstd, in_=r, func=AF.Exp, scale=-0.5),
                         reads=[rres], writes=[rsres])
                    rs[tt] = (rstd, rsres)
                for tt in range(2):
                    hs = slice(tt * TB, (tt + 1) * TB)
                    tb = 2 * hf + tt
                    c = cols(tb)
                    rstd, rsres = rs[tt]
                    for k in range(NCH):
                        tmp, tres = NT.tmp.next()
                        S.op("dve", lambda e, tmp=tmp, k=k, hs=hs, rstd=rstd: e.tensor_tensor(
                            out=tmp, in0=yTh[:, k, hs], in1=rstd, op=ALU.mult), reads=[("yTh", k, tt), rsres], writes=[tres])
                        S.op("dve", lambda e, tmp=tmp, k=k, c=c: e.scalar_tensor_tensor(
                            out=xT[:, k, c], in0=tmp, scalar=G2[:, k:k + 1], in1=xT[:, k, c], op0=ALU.mult, op1=ALU.add),
                             reads=[tres, ("x", k, tb)] + list(vr), writes=[("x", k, tb)])

            ffn_pre(0)
            ffn_up(0)
            side.extend(pre_groups(1))
            ffn_down(0)
            side_flush()
            side.extend(post_groups(0))
            ffn_up(1)
            side_flush()
            ffn_down_last(1)
            if do_ada:
                while ada_s < 24:
                    emit_ada_slice(l + 1, ada_s, WA, adaps, adares)
                    ada_s += 1
                finish_ada(l + 1, adaps, adares)
            if DEBUG and l == 0:
                S.dma("sp", lambda e: e.dma_start(out=dbg_x, in_=xT.rearrange("p k t -> p (k t)")),
                      reads=[("x", k, tb) for k in range(NCH) for tb in range(NTB)], key=("dbgx", 0))

        for l_ in range(n_layers):
            layer_body(l_)

        yout_v = yT_out.rearrange("(k p) t -> p k t", p=128)
        for tb in range(NTB):
            c = cols(tb)
            for k in range(NCH):
                S.dma("sp", lambda e, c=c, k=k: e.dma_start(out=yout_v[:, k, c], in_=xT[:, k, c]),
                      reads=[("x", k, tb)], key=("xout", k), phase=False)
        S.final_wait("sp")
        S.emit()
    return nc


def _role_tables(is_sample):
    rope = np.zeros((2, 128, T), np.float32)
    if is_sample:
        t = np.arange(T)
        row = (t // 64).astype(np.float32)
        col = (t % 64).astype(np.float32)
        inv = (10000.0 ** (-np.arange(32, dtype=np.float32) / 32)).astype(np.float32)
        ang_r = row[None, :] * inv[:, None]
        ang_c = col[None, :] * inv[:, None]
        for base, ang in ((0, ang_r), (64, ang_c)):
            rope[0, base:base + 32] = np.cos(ang)
            rope[0, base + 32:base + 64] = np.cos(ang)
            rope[1, base:base + 32] = -np.sin(ang)
            rope[1, base + 32:base + 64] = np.sin(ang)
    else:
        rope[0] = 1.0
    mask = np.zeros((128, 3, TB), np.float32)
    if is_sample:
        mask[:] = 1.0
    else:
        mask[:, 0, 0:256] = 1.0
        mask[:, 1, 256:512] = 1.0
    flag = np.full((128, 1), 1.0 if is_sample else 0.0, np.float32)
    negb = np.full((128, 1), 0.0 if is_sample else -30000.0, np.float32)
    L = T if is_sample else 256
    pos = np.arange(T) % L
    invcnt = np.zeros((4, 128, T), np.float32)
    for g, w in enumerate(POOL_W):
        lo = np.clip(pos - w // 2, 0, L)
        hi = np.clip(pos + w - w // 2, 0, L)
        invcnt[g] = (1.0 / (hi - lo).astype(np.float32))[None, :]
    return rope, mask.reshape(128, 3 * TB), flag, invcnt, negb


def _fm(v):
    v = np.asarray(v, np.float32)
    lead = v.shape[:-1]
    return np.ascontiguousarray(np.moveaxis(v.reshape(lead + (8, 128)), -1, 0))


_NC_CACHE = {}


def kernel(x_prompt, x_sample, c, cache_k, cache_v, c_ctx, w_ada, b_ada, norm_gains,
           w_qkv, qk_gains, w_o, w_pool, pool_scale, w_up, w_down, _n_layers=DEPTH):
    f = lambda a: np.ascontiguousarray(np.asarray(a, dtype=np.float32))
    x_prompt, x_sample, c, cache_k, cache_v, c_ctx = map(f, (x_prompt, x_sample, c, cache_k, cache_v, c_ctx))
    w_ada, b_ada, norm_gains, w_qkv, qk_gains, w_o, w_pool, pool_scale, w_up, w_down = map(
        f, (w_ada, b_ada, norm_gains, w_qkv, qk_gains, w_o, w_pool, pool_scale, w_up, w_down))
    if _n_layers not in _NC_CACHE:
        _NC_CACHE[_n_layers] = build_program(_n_layers)
    nc = _NC_CACHE[_n_layers]

    badaT = np.ascontiguousarray(b_ada.reshape(DEPTH, 48, 128).transpose(2, 0, 1).reshape(128, DEPTH * 48))
    gT = _fm(norm_gains).reshape(128, DEPTH * 4 * 8)
    qkgT = np.ascontiguousarray(qk_gains.reshape(4, 128).T)
    pscT = _fm(pool_scale).reshape(128, 16)
    rperm = np.zeros((128, 128), np.float32)
    for d in range(128):
        rperm[d ^ 32, d] = 1.0
    shared = dict(w_ada=w_ada, badaT=badaT, gT=gT, w_qkv=w_qkv, qkgT=qkgT, w_o=w_o, w_pool=w_pool,
                  pscT=pscT, w_up=w_up, w_down=w_down, rperm=rperm)
    tabs = {False: _role_tables(False), True: _role_tables(True)}
    zeroKT = np.zeros((2, 2, 128, PAST), np.float32)
    zeroV = np.zeros((2, PAST, 256), np.float32)
    in_maps = []
    for core in range(8):
        is_s = core >= 4
        rope, mask, flag, invcnt, negb = tabs[is_s]
        m = dict(shared)
        if is_s:
            b = core - 4
            m["xT_in"] = np.ascontiguousarray(x_sample[b].T)
            m["cond"] = _fm(c[b])
            m["cacheKT"] = np.ascontiguousarray(cache_k[b].transpose(0, 2, 3, 1))
            m["cacheV"] = np.ascontiguousarray(cache_v[b].reshape(2, PAST, 256))
        else:
            xs = x_prompt[core * 8:(core + 1) * 8].reshape(8 * 256, D)
            m["xT_in"] = np.ascontiguousarray(xs.T)
            m["cond"] = _fm(c_ctx)
            m["cacheKT"] = zeroKT
            m["cacheV"] = zeroV
        m["rope"], m["maskT"], m["flag"], m["invcnt"], m["negb"] = rope, mask, flag, invcnt, negb
        in_maps.append(m)
    res = run_bass_kernel_spmd(nc, in_maps, core_ids=list(range(8)))
    R = res.results
    if DEBUG:
        DBG["res"] = R
    y_prompt = np.empty((32, 256, D), np.float32)
    y_sample = np.empty((4, T, D), np.float32)
    new_k = np.empty((32, 2, 256, 2, 128), np.float32)
    new_v = np.empty((32, 2, 256, 2, 128), np.float32)
    for core in range(8):
        yT = np.asarray(R[core]["yT_out"])
        if core >= 4:
            y_sample[core - 4] = yT.T
        else:
            y_prompt[core * 8:(core + 1) * 8] = yT.T.reshape(8, 256, D)
            nk = np.asarray(R[core]["newkT"]).reshape(2, 2, 128, 8, 256)
            new_k[core * 8:(core + 1) * 8] = nk.transpose(3, 0, 4, 1, 2)
            nv = np.asarray(R[core]["newv"]).reshape(2, 8, 256, 2, 128)
            new_v[core * 8:(core + 1) * 8] = nv.transpose(1, 0, 2, 3, 4)
    return (y_prompt, y_sample, new_k, new_v)
```

```python
import numpy as np
from contextlib import ExitStack
import concourse.bass as bass
import concourse.mybir as mybir
from concourse.bass_utils import run_bass_kernel_spmd

F32 = mybir.dt.float32
BF16 = mybir.dt.bfloat16
AF = mybir.ActivationFunctionType
ALU = mybir.AluOpType

D = 1024
NCH = 8
T = 2048
TB = 512
NTB = 4
DEPTH = 4
DFF = 2816
NF = 22
EPS = 1e-6
PAST = 512
NKT = 20
POOL_W = (2, 4, 8, 16)
ARENA_BYTES = 142848
DEBUG = False
DBG = {}


class Sched:
    ENGS = ("pe", "act", "dve", "pool", "sp")

    def __init__(self, nc, stack):
        self.nc = nc
        self.stack = stack
        self.ops = {e: [] for e in self.ENGS}
        self.count = {e: 0 for e in self.ENGS}
        self.seen = {e: {} for e in self.ENGS}
        self.sems = {}
        for e in self.ENGS:
            self.sems[e] = stack.enter_context(nc.semaphore("s_" + e))
        self.dma_val = {}
        self.res_w = {}
        self.res_r = {}

    def _sem(self, key):
        if key not in self.sems:
            self.sems[key] = self.stack.enter_context(self.nc.semaphore("d_%d" % len(self.sems)))
            self.dma_val[key] = 0
        return self.sems[key]

    def _deps(self, E, reads, writes):
        need = {}

        def add(tok, raw):
            if tok is None:
                return
            k, v = tok
            if k == E and E in ("pe", "sp"):
                return
            if self.seen[E].get(k, 0) >= v:
                return
            if need.get(k, 0) < v:
                need[k] = v

        for r in reads:
            add(self.res_w.get(r), True)
        for w in writes:
            add(self.res_w.get(w), False)
            for k, v in self.res_r.get(w, {}).items():
                add((k, v), False)
        for k, v in need.items():
            self.seen[E][k] = v
        return list(need.items())

    def op(self, E, fns, reads=(), writes=(), phase=True):
        if callable(fns):
            fns = [fns]
        reads = list(reads) + (["phase"] if phase else [])
        waits = self._deps(E, reads, writes)
        self.count[E] += 1
        idx = self.count[E]
        self.ops[E].append((waits, fns, (E, 1)))
        for w in writes:
            self.res_w[w] = (E, idx)
            self.res_r[w] = {}
        for r in reads:
            self.res_r.setdefault(r, {})[E] = idx

    def dma(self, Q, fn, reads=(), writes=(), key=None, phase=True):
        if key is None:
            key = ("dma", (writes[0] if writes else reads[0]))
        self._sem(key)
        reads = list(reads) + (["phase"] if phase else [])
        waits = self._deps(Q, reads, writes)
        self.dma_val[key] += 16
        tok = (key, self.dma_val[key])
        self.ops[Q].append((waits, [fn], (key, 16)))
        for w in writes:
            self.res_w[w] = tok
            self.res_r[w] = {}
        for r in reads:
            self.res_r.setdefault(r, {})[key] = tok[1]

    def final_wait(self, E="sp"):
        waits = []
        for k, v in self.dma_val.items():
            if self.seen[E].get(k, 0) < v:
                waits.append((k, v))
        for e in self.ENGS:
            if e != E and self.count[e] > 0 and self.seen[E].get(e, 0) < self.count[e]:
                waits.append((e, self.count[e]))
        self.ops[E].append((waits, [], None))

    def emit(self):
        nc = self.nc
        with nc.Block() as block:
            def run(E):
                def body(engine):
                    for waits, fns, inc in self.ops[E]:
                        for k, v in waits:
                            engine.wait_ge(self.sems[k], v)
                        last = None
                        for f in fns:
                            last = f(engine)
                        if inc is not None:
                            last.then_inc(self.sems[inc[0]], inc[1])
                return body
            block.tensor(run("pe"))
            block.scalar(run("act"))
            block.vector(run("dve"))
            block.gpsimd(run("pool"))
            block.sync(run("sp"))


class Rot:
    def __init__(self, name, aps):
        self.name = name
        self.aps = aps
        self.i = -1

    def next(self):
        self.i = (self.i + 1) % len(self.aps)
        return self.aps[self.i], (self.name, self.i)


class Arena:
    def __init__(self, ap):
        self.ap = ap
        self.off = 0

    def reset(self):
        self.off = 0

    def f32(self, n):
        assert self.off % 4 == 0
        a = self.ap[:, self.off // 4: self.off // 4 + n]
        self.off += 4 * n
        assert self.off <= ARENA_BYTES, self.off
        return a

    def bf16(self, n):
        assert n % 2 == 0
        a = self.f32(n // 2).bitcast(BF16)
        return a


def build_program(n_layers=DEPTH):
    nc = bass.Bass("TRN2", target_bir_lowering=False)

    def din(name, shape):
        return nc.dram_tensor(name, list(shape), F32, kind="ExternalInput").ap()

    def dout(name, shape):
        return nc.dram_tensor(name, list(shape), F32, kind="ExternalOutput").ap()

    xT_in = din("xT_in", [D, T])
    cond_in = din("cond", [128, 8])
    w_ada = din("w_ada", [DEPTH, D, 6 * D])
    bada_in = din("badaT", [128, DEPTH * 48])
    g_in = din("gT", [128, DEPTH * 4 * 8])
    w_qkv = din("w_qkv", [2, D, 1536])
    qkg_in = din("qkgT", [128, 4])
    w_o = din("w_o", [2, D, D])
    w_pool = din("w_pool", [2, 4, 256, 256])
    psc_in = din("pscT", [128, 16])
    w_up = din("w_up", [DEPTH, D, 2 * DFF])
    w_down = din("w_down", [DEPTH, DFF, D])
    cacheKT = din("cacheKT", [2, 2, 128, PAST])
    cacheV = din("cacheV", [2, PAST, 256])
    rope_in = din("rope", [2, 128, T])
    mask_in = din("maskT", [128, 3 * TB])
    flag_in = din("flag", [128, 1])
    negb_in = din("negb", [128, 1])
    invcnt_in = din("invcnt", [4, 128, T])
    rperm_in = din("rperm", [128, 128])

    yT_out = dout("yT_out", [D, T])
    newkT = dout("newkT", [2, 256, T])
    newv = dout("newv", [2, T, 256])
    if DEBUG:
        dbg_ada = dout("dbg_ada", [DEPTH, 128, 48])
        dbg_y = dout("dbg_y", [128, NCH * TB])
        dbg_hc = dout("dbg_hc", [2, 128, 8 * 272])
        dbg_x = dout("dbg_x", [128, NCH * T])
        dbg_fin = dout("dbg_fin", [2, 128, 8 * 272])
        dbg_tmp2 = dout("dbg_tmp2", [2, 128, T])
        dbg_pooled = dout("dbg_pooled", [2, 128, T])
        dbg_tmpf = dout("dbg_tmpf", [2, 128, T])

    with ExitStack() as st:
        S = Sched(nc, st)

        def sb(name, shape, dt):
            return st.enter_context(nc.sbuf_tensor(name, list(shape), dt))

        xT = sb("xT", [128, NCH, T], F32)
        ones_n = sb("ones_n", [128, 128], BF16)
        ones_h = sb("ones_h", [128, 128], BF16)
        ones_1 = sb("ones_1", [128, 128], BF16)
        rperm = sb("rperm_b", [128, 128], BF16)
        gT = sb("gT_s", [128, DEPTH * 4 * 8], F32)
        qkgT = sb("qkgT_s", [128, 4], F32)
        pscT = sb("pscT_s", [128, 16], F32)
        badaT = sb("badaT_s", [128, DEPTH * 48], F32)
        cond = sb("cond_s", [128, 8], F32)
        scond = sb("scond", [128, 8], F32)
        sc_hi = sb("sc_hi", [128, 8], BF16)
        sc_hf = sb("sc_hf", [128, 8], F32)
        sc_lo = sb("sc_lo", [128, 8], BF16)
        negb = sb("negb_s", [128, 1], F32)
        ones_f = sb("ones_f", [128, 128], F32)
        adaT = [sb("adaT%d" % i, [128, 48], F32) for i in range(2)]
        vecs = [sb("vecs%d" % i, [128, 4, 8], F32) for i in range(2)]
        flag = sb("flag_s", [128, 1], F32)
        dummy = sb("dummy_bar", [128, 8], F32)
        arena_t = sb("arena", [128, ARENA_BYTES // 4], F32)
        AR = Arena(arena_t[:, :])
        psum = st.enter_context(nc.psum_tensor("psum", [128, 8, TB], F32))

        def PS(b):
            return psum[:, b, :], ("ps", b)

        def phase_barrier():
            S.op("dve", lambda e: e.memset(dummy[:], 0.0), writes=["phase"], phase=False)

        S.op("dve", lambda e: e.memset(ones_n[:], 1.0 / D), writes=["ones_n"], phase=False)
        S.op("dve", lambda e: e.memset(ones_h[:], 1.0 / 128), writes=["ones_h"], phase=False)
        S.op("dve", lambda e: e.memset(ones_1[:], 1.0), writes=["ones_1"], phase=False)
        S.op("dve", lambda e: e.memset(ones_f[:], 1.0), writes=["ones_f"], phase=False)
        S.dma("pool", lambda e: e.dma_start(out=rperm[:], in_=rperm_in), writes=["rperm"], phase=False)
        for nm, dst, src in (("gT", gT, g_in), ("qkgT", qkgT, qkg_in), ("pscT", pscT, psc_in),
                             ("badaT", badaT, bada_in), ("cond", cond, cond_in), ("flag", flag, flag_in),
                             ("negb", negb, negb_in)):
            S.dma("sp", lambda e, dst=dst, src=src: e.dma_start(out=dst[:], in_=src), writes=[nm], phase=False)
        xin_v = xT_in.rearrange("(k p) t -> p k t", p=128)
        for k in range(NCH):
            S.dma("sp", lambda e, k=k: e.dma_start(out=xT[:, k, :], in_=xin_v[:, k, :]),
                  writes=[("x", k, tb) for tb in range(NTB)], key=("xin", k), phase=False)
        S.op("act", lambda e: e.activation(out=scond[:], in_=cond[:], func=AF.Silu),
             reads=["cond"], writes=["scond"], phase=False)
        S.op("dve", lambda e: e.tensor_copy(out=sc_hi[:], in_=scond[:]), reads=["scond"], writes=["sc_hi"], phase=False)
        S.op("dve", lambda e: e.tensor_copy(out=sc_hf[:], in_=sc_hi[:]), reads=["sc_hi"], writes=["sc_hf"], phase=False)
        S.op("dve", lambda e: e.tensor_tensor(out=sc_lo[:], in0=scond[:], in1=sc_hf[:], op=ALU.subtract),
             reads=["scond", "sc_hf"], writes=["sc_lo"], phase=False)

        def cols(tb):
            return slice(tb * TB, (tb + 1) * TB)

        class NormTemps:
            def __init__(self):
                self.sq = Rot("sq", [AR.bf16(TB) for _ in range(2)])
                self.r = Rot("r", [AR.f32(TB) for _ in range(1)])
                self.rstd = Rot("rstd", [AR.f32(TB) for _ in range(2)])
                self.tmp = Rot("tmp", [AR.f32(TB) for _ in range(2)])

        def mean_rstd(NT, srcs, src_res, bank, ones, ones_res):
            mps, mres = PS(bank)
            n = len(srcs)
            for i, (s_ap, s_res) in enumerate(zip(srcs, src_res)):
                sq, sqres = NT.sq.next()
                S.op("act", lambda e, sq=sq, s_ap=s_ap: e.activation(out=sq, in_=s_ap, func=AF.Square),
                     reads=[s_res], writes=[sqres])
                S.op("pe", lambda e, sq=sq, i=i: e.matmul(mps, lhsT=ones[:], rhs=sq, start=(i == 0), stop=(i == n - 1)),
                     reads=[sqres, ones_res], writes=[mres])
            r, rres = NT.r.next()
            S.op("act", lambda e: e.activation(out=r, in_=mps, func=AF.Ln, bias=EPS, scale=1.0),
                 reads=[mres], writes=[rres])
            rstd, rsres = NT.rstd.next()
            S.op("act", lambda e: e.activation(out=rstd, in_=r, func=AF.Exp, scale=-0.5), reads=[rres], writes=[rsres])
            return rstd, rsres

        def pre_norm(NT, tb, bank, Av, Bv, vres, outs, out_res):
            c = cols(tb)
            rstd, rsres = mean_rstd(NT, [xT[:, k, c] for k in range(NCH)], [("x", k, tb) for k in range(NCH)],
                                    bank, ones_n, "ones_n")
            for k in range(NCH):
                tmp, tres = NT.tmp.next()
                S.op("dve", lambda e, tmp=tmp, k=k: e.tensor_tensor(out=tmp, in0=xT[:, k, c], in1=rstd, op=ALU.mult),
                     reads=[("x", k, tb), rsres], writes=[tres])
                S.op("dve", lambda e, tmp=tmp, k=k: e.tensor_scalar(out=outs[k], in0=tmp, scalar1=Av[:, k:k + 1],
                                                                    scalar2=Bv[:, k:k + 1], op0=ALU.mult, op1=ALU.add),
                     reads=[tres] + list(vres), writes=[out_res[k]])

        def post_norm(NT, tb, bank, ys, y_res, Gv, vres):
            c = cols(tb)
            rstd, rsres = mean_rstd(NT, ys, y_res, bank, ones_n, "ones_n")
            for k in range(NCH):
                tmp, tres = NT.tmp.next()
                S.op("dve", lambda e, tmp=tmp, k=k: e.tensor_tensor(out=tmp, in0=ys[k], in1=rstd, op=ALU.mult),
                     reads=[y_res[k], rsres], writes=[tres])
                S.op("dve", lambda e, tmp=tmp, k=k: e.scalar_tensor_tensor(
                    out=xT[:, k, c], in0=tmp, scalar=Gv[:, k:k + 1], in1=xT[:, k, c], op0=ALU.mult, op1=ALU.add),
                     reads=[tres, ("x", k, tb)] + list(vres), writes=[("x", k, tb)])

        def norm_groups(NT, tb, bank, srcs, src_res, kind, Av=None, Bv=None, Gv=None, vres=(), outs=None, out_res=None):
            c = cols(tb)
            mps, mres = PS(bank)
            st_ = {}

            def sq_a(k):
                sq, sqres = NT.sq.next()
                st_[("sq", k)] = (sq, sqres)
                S.op("act", lambda e: e.activation(out=sq, in_=srcs[k], func=AF.Square), reads=[src_res[k]], writes=[sqres])

            def sq_b(k):
                sq, sqres = st_.pop(("sq", k))
                S.op("pe", lambda e: e.matmul(mps, lhsT=ones_n[:], rhs=sq, start=(k == 0), stop=(k == NCH - 1)),
                     reads=[sqres, "ones_n"], writes=[mres])

            def rstd_p():
                r, rres = NT.r.next()
                S.op("act", lambda e: e.activation(out=r, in_=mps, func=AF.Ln, bias=EPS, scale=1.0), reads=[mres], writes=[rres])
                rstd, rsres = NT.rstd.next()
                S.op("act", lambda e: e.activation(out=rstd, in_=r, func=AF.Exp, scale=-0.5), reads=[rres], writes=[rsres])
                st_["rstd"] = (rstd, rsres)

            def fin(k):
                rstd, rsres = st_["rstd"]
                tmp, tres = NT.tmp.next()
                S.op("dve", lambda e: e.tensor_tensor(out=tmp, in0=srcs[k], in1=rstd, op=ALU.mult),
                     reads=[src_res[k], rsres], writes=[tres])
                if kind == "pre":
                    S.op("dve", lambda e: e.tensor_scalar(out=outs[k], in0=tmp, scalar1=Av[:, k:k + 1], scalar2=Bv[:, k:k + 1],
                                                          op0=ALU.mult, op1=ALU.add),
                         reads=[tres] + list(vres), writes=[out_res[k]])
                else:
                    S.op("dve", lambda e: e.scalar_tensor_tensor(
                        out=xT[:, k, c], in0=tmp, scalar=Gv[:, k:k + 1], in1=xT[:, k, c], op0=ALU.mult, op1=ALU.add),
                         reads=[tres, ("x", k, tb)] + list(vres), writes=[("x", k, tb)])
            groups = []
            for t in range(NCH // 2 + 1):
                g = []
                for k in (2 * t - 2, 2 * t - 1):
                    if 0 <= k < NCH:
                        g.append(lambda k=k: sq_b(k))
                for k in (2 * t, 2 * t + 1):
                    if 0 <= k < NCH:
                        g.append(lambda k=k: sq_a(k))
                groups.append(g)
            groups.append([rstd_p])
            for t in range(2):
                groups.append([lambda k=k: fin(k) for k in range(4 * t, 4 * t + 4)])
            return groups

        def emit_ada_slice(l, s, WA, adaps, adares):
            wa, wres = WA.next()
            src = w_ada[l].rearrange("(k p) n -> p k n", p=128)[:, :, s * 256:(s + 1) * 256]
            S.dma("pool", lambda e: e.dma_start(out=wa, in_=src), writes=[wres])
            fns = []
            for jj in range(2):
                j = 2 * s + jj
                for k in range(NCH):
                    for hl, sc in enumerate((sc_hi, sc_lo)):
                        fns.append(lambda e, j=j, jj=jj, k=k, hl=hl, sc=sc: e.matmul(
                            adaps[:, j:j + 1], lhsT=wa[:, k, jj * 128:(jj + 1) * 128], rhs=sc[:, k:k + 1],
                            start=(k == 0 and hl == 0), stop=(k == NCH - 1 and hl == 1)))
            S.op("pe", fns, reads=[wres, "sc_hi", "sc_lo"], writes=[adares])

        def finish_ada(l, adaps, adares, part=None):
            a = adaT[l % 2]
            v = vecs[l % 2]
            c0, c1 = {None: (0, 48), "A": (0, 24), "B": (24, 48)}[part]
            S.op("dve", lambda e: e.tensor_tensor(out=a[:, c0:c1], in0=adaps[:, c0:c1],
                                                  in1=badaT[:, l * 48 + c0:l * 48 + c1], op=ALU.add),
                 reads=[adares, "badaT"], writes=[("ada", l % 2)], phase=False)
            if DEBUG and part in (None, "B"):
                S.dma("sp", lambda e: e.dma_start(out=dbg_ada[l], in_=a[:]), reads=[("ada", l % 2)], key=("dbgada", l),
                      phase=False)
            gb = l * 32
            if part in (None, "A"):
                S.op("dve", lambda e: e.scalar_tensor_tensor(out=v[:, 0, :], in0=a[:, 8:16], scalar=1.0,
                                                             in1=gT[:, gb:gb + 8], op0=ALU.add, op1=ALU.mult),
                     reads=[("ada", l % 2), "gT"], writes=[("vec", l % 2, 0)], phase=False)
                S.op("dve", lambda e: e.tensor_tensor(out=v[:, 1, :], in0=a[:, 16:24], in1=gT[:, gb + 8:gb + 16], op=ALU.mult),
                     reads=[("ada", l % 2), "gT"], writes=[("vec", l % 2, 1)], phase=False)
            if part in (None, "B"):
                S.op("dve", lambda e: e.scalar_tensor_tensor(out=v[:, 2, :], in0=a[:, 32:40], scalar=1.0,
                                                             in1=gT[:, gb + 16:gb + 24], op0=ALU.add, op1=ALU.mult),
                     reads=[("ada", l % 2), "gT"], writes=[("vec", l % 2, 2)], phase=False)
                S.op("dve", lambda e: e.tensor_tensor(out=v[:, 3, :], in0=a[:, 40:48], in1=gT[:, gb + 24:gb + 32], op=ALU.mult),
                     reads=[("ada", l % 2), "gT"], writes=[("vec", l % 2, 3)], phase=False)

        phase_barrier()
        AR.reset()
        WA0 = Rot("wa", [AR.bf16(8 * 256).rearrange("p (k n) -> p k n", k=8) for _ in range(3)])
        adaps0, adares0 = PS(7)
        for s in range(12):
            emit_ada_slice(0, s, WA0, adaps0, adares0)
        finish_ada(0, adaps0, adares0, part="A")

        def layer_body(l):
            a_T = adaT[l % 2]
            v_T = vecs[l % 2]
            A1, G1, A2, G2 = (v_T[:, i, :] for i in range(4))
            B1 = a_T[:, 0:8]
            B2 = a_T[:, 24:32]
            vr = [("ada", l % 2)] + [("vec", l % 2, i) for i in range(4)]

            if l % 2 == 0:
                a = l // 2
                phase_barrier()
                AR.reset()
                hT = AR.bf16(NCH * T).rearrange("p (k t) -> p k t", k=NCH)
                QT = AR.bf16(NCH * T).rearrange("p (k t) -> p k t", k=NCH)
                KT = AR.bf16(2 * (T + PAST)).rearrange("p (k t) -> p k t", k=2)
                Vb = AR.bf16(NKT * 256).rearrange("p (k c) -> p k c", k=NKT)
                mark = AR.off
                WQ = Rot("wq", [AR.bf16(NCH * 128).rearrange("p (k n) -> p k n", k=NCH) for _ in range(3)])
                NT = NormTemps()
                qn_r = Rot("qn", [AR.f32(TB) for _ in range(2)])
                qnb_r = Rot("qnb", [AR.bf16(TB) for _ in range(2)])
                t1_r = Rot("t1", [AR.f32(TB) for _ in range(2)])
                t2_r = Rot("t2", [AR.f32(TB) for _ in range(2)])
                rope_r = Rot("ropeb", [AR.f32(2 * TB).rearrange("p (a n) -> p a n", a=2) for _ in range(2)])
                vst_r = Rot("vst", [AR.f32(128) for _ in range(2)])

                for kvh in range(2):
                    S.dma("pool", lambda e, kvh=kvh: e.dma_start(out=KT[:, kvh, T:T + PAST], in_=cacheKT[a, kvh]),
                          writes=[("KTc", kvh)])
                S.dma("pool", lambda e: e.dma_start(out=Vb[:, 16:20, :],
                                                    in_=cacheV[a].rearrange("(i p) c -> p i c", p=128)),
                      writes=["Vc"])

                wq_v = w_qkv[a].rearrange("(k p) n -> p k n", p=128)
                wq_of = {}

                def load_wq(j):
                    wq, wqres = WQ.next()
                    c0 = j * 128 if j < 10 else 1280 + (j - 10) * 128
                    S.dma("pool", lambda e: e.dma_start(out=wq, in_=wq_v[:, :, c0:c0 + 128]), writes=[wqres])
                    wq_of[j] = (wq, wqres)

                load_wq(0)
                load_wq(1)
                for tb in range(NTB):
                    pre_norm(NT, tb, 3, A1, B1, vr, [hT[:, k, cols(tb)] for k in range(NCH)],
                             [("hT", k, tb) for k in range(NCH)])

                QK = [(j, tb) for j in range(10) for tb in range(NTB)]
                stt_ = {}
                qn3_r = Rot("qn", [qn_r.aps[0], qn_r.aps[1], AR.f32(TB)])

                def stA(i):
                    j, tb = QK[i]
                    if tb == 0 and j + 2 < 12:
                        load_wq(j + 2)
                    wq, wqres = wq_of[j]
                    c = cols(tb)
                    qps, qres = PS(i % 3)
                    S.op("pe", [lambda e, k=k: e.matmul(qps, lhsT=wq[:, k, :], rhs=hT[:, k, c],
                                                        start=(k == 0), stop=(k == NCH - 1)) for k in range(NCH)],
                         reads=[wqres] + [("hT", k, tb) for k in range(NCH)], writes=[qres])
                    stt_[i] = dict(qps=qps, qres=qres)

                def stB1(i):
                    d = stt_[i]
                    qps, qres = d["qps"], d["qres"]
                    sq, sqres = NT.sq.next()
                    mps, mres = PS(3 if i % 2 == 0 else 6)
                    S.op("act", lambda e: e.activation(out=sq, in_=qps, func=AF.Square), reads=[qres], writes=[sqres])
                    S.op("pe", lambda e: e.matmul(mps, lhsT=ones_h[:], rhs=sq, start=True, stop=True),
                         reads=[sqres, "ones_h"], writes=[mres])
                    d.update(mps=mps, mres=mres)

                def stB2(i):
                    j, tb = QK[i]
                    d = stt_[i]
                    qps, qres, mps, mres = d["qps"], d["qres"], d["mps"], d["mres"]
                    gcol = a * 2 + (0 if j < 8 else 1)
                    r, rres = NT.r.next()
                    S.op("act", lambda e: e.activation(out=r, in_=mps, func=AF.Ln, bias=EPS, scale=1.0),
                         reads=[mres], writes=[rres])
                    rstd, rsres = NT.rstd.next()
                    S.op("act", lambda e: e.activation(out=rstd, in_=r, func=AF.Exp, scale=-0.5), reads=[rres], writes=[rsres])
                    qn, qnres = qn3_r.next()
                    S.op("dve", lambda e: e.scalar_tensor_tensor(
                        out=qn, in0=qps, scalar=qkgT[:, gcol:gcol + 1], in1=rstd, op0=ALU.mult, op1=ALU.mult),
                         reads=[qres, rsres, "qkgT"], writes=[qnres])
                    d.update(qn=qn, qnres=qnres)

                def stB3(i):
                    j, tb = QK[i]
                    d = stt_[i]
                    c = cols(tb)
                    qn, qnres = d["qn"], d["qnres"]
                    if j >= 8:
                        S.dma("sp", lambda e: e.dma_start(out=newkT[a, (j - 8) * 128:(j - 7) * 128, c], in_=qn),
                              reads=[qnres])
                    qnb, qnbres = qnb_r.next()
                    S.op("act", lambda e: e.activation(out=qnb, in_=qn, func=AF.Copy), reads=[qnres], writes=[qnbres])
                    rp, rpres = rope_r.next()
                    S.dma("sp", lambda e: e.dma_start(out=rp, in_=rope_in[:, :, c].rearrange("a p n -> p a n")),
                          writes=[rpres])
                    d.update(qnb=qnb, qnbres=qnbres, rp=rp, rpres=rpres)

                def stC(i):
                    j, tb = QK[i]
                    d = stt_.pop(i)
                    c = cols(tb)
                    qn, qnres, qnb, qnbres, rp, rpres = d["qn"], d["qnres"], d["qnb"], d["qnbres"], d["rp"], d["rpres"]
                    rps, rres = PS(4 + i % 2)
                    S.op("pe", lambda e: e.matmul(rps, lhsT=rperm[:], rhs=qnb, start=True, stop=True),
                         reads=[qnbres, "rperm"], writes=[rres])
                    t1, t1res = t1_r.next()
                    t2, t2res = t2_r.next()
                    S.op("dve", lambda e: e.tensor_tensor(out=t1, in0=qn, in1=rp[:, 0, :], op=ALU.mult),
                         reads=[qnres, rpres], writes=[t1res])
                    S.op("dve", lambda e: e.tensor_tensor(out=t2, in0=rps, in1=rp[:, 1, :], op=ALU.mult),
                         reads=[rres, rpres], writes=[t2res])
                    if j < 8:
                        dst, dres = QT[:, j, c], ("QT", j, tb)
                    else:
                        dst, dres = KT[:, j - 8, c], ("KT", j - 8, tb)
                    S.op("dve", lambda e: e.tensor_tensor(out=dst, in0=t1, in1=t2, op=ALU.add),
                         reads=[t1res, t2res], writes=[dres])

                nq = len(QK)
                stages = (stA, stB1, stB2, stB3, stC)
                ada_q = 12 if l == 0 else 24
                if l == 0:
                    WAq = Rot("wa", [AR.bf16(8 * 256).rearrange("p (k n) -> p k n", k=8) for _ in range(2)])
                    adapsq, adaresq = PS(7)
                for it in range(nq + len(stages) - 1):
                    for si, fn in enumerate(stages):
                        if 0 <= it - si < nq:
                            fn(it - si)
                    if ada_q < 24 and it >= 2 and it % 3 == 2:
                        emit_ada_slice(0, ada_q, WAq, adapsq, adaresq)
                        ada_q += 1
                if l == 0:
                    while ada_q < 24:
                        emit_ada_slice(0, ada_q, WAq, adapsq, adaresq)
                        ada_q += 1
                    finish_ada(0, adapsq, adaresq, part="B")
                vi = 0
                for vs in range(2):
                    wq, wqres = wq_of[10 + vs]
                    for kt in range(16):
                        vps, vres_ = PS(3 if vi % 2 == 0 else 6)
                        vi += 1
                        tbk = kt // 4
                        S.op("pe", [lambda e, k=k, wq=wq, vps=vps, kt=kt: e.matmul(
                            vps[:, 0:128], lhsT=hT[:, k, kt * 128:(kt + 1) * 128], rhs=wq[:, k, :],
                            start=(k == 0), stop=(k == NCH - 1)) for k in range(NCH)],
                             reads=[wqres] + [("hT", k, tbk) for k in range(NCH)], writes=[vres_])
                        vst, vstres = vst_r.next()
                        S.op("act", lambda e, vst=vst, vps=vps: e.activation(out=vst, in_=vps[:, 0:128], func=AF.Copy),
                             reads=[vres_], writes=[vstres])
                        S.dma("sp", lambda e, vst=vst, kt=kt, vs=vs: e.dma_start(
                            out=newv[a, kt * 128:(kt + 1) * 128, vs * 128:(vs + 1) * 128], in_=vst), reads=[vstres])
                        S.op("pool", lambda e, vst=vst, kt=kt, vs=vs: e.tensor_copy(
                            out=Vb[:, kt, vs * 128:(vs + 1) * 128], in_=vst), reads=[vstres], writes=[("Vb", kt, vs)])

                phase_barrier()
                AR.reset()
                WO = AR.bf16(NCH * D).rearrange("p (k n) -> p k n", k=NCH)
                yT = AR.f32(NCH * TB).rearrange("p (k t) -> p k t", k=NCH)
                assert AR.off <= 32768
                AR.off = mark
                wo_v = w_o[a].rearrange("(k p) n -> p k n", p=128)
                for k in range(NCH):
                    S.dma("pool", lambda e, k=k: e.dma_start(out=WO[:, k, :], in_=wo_v[:, k, :]), writes=[("WO", k)])
                PT_r = Rot("PT", [AR.bf16(2 * TB).rearrange("p (a n) -> p a n", a=2) for _ in range(6)])
                tp_r = Rot("tpair", [AR.bf16(TB) for _ in range(6)])
                acc_r = Rot("acc", [AR.f32(TB) for _ in range(2)])
                accb_r = Rot("accb", [AR.bf16(TB) for _ in range(2)])
                rec_r = Rot("rec", [AR.f32(TB) for _ in range(2)])
                lnr_r = Rot("lnr", [AR.f32(TB) for _ in range(1)])
                NT = NormTemps()
                scale = 1.0 / float(np.sqrt(128.0))
                NKP = NKT // 2
                pairs = [(qb, h, kp) for qb in range(NTB) for h in range(NCH) for kp in range(NKP)]
                pend = {}
                accs = {}
                hold = {}

                def emit_S(i):
                    qb, h, kp = pairs[i]
                    kvh = h // 4
                    slot = i % 2
                    sp2 = psum[:, 2 * slot:2 * slot + 2, :]
                    sres = [("ps", 2 * slot), ("ps", 2 * slot + 1)]
                    kres = [("KT", kvh, kp // 2)] if kp < 8 else [("KTc", kvh)]
                    S.op("pe", [lambda e, u=u: e.matmul(psum[:, 2 * slot + u, :],
                                                        lhsT=KT[:, kvh, (2 * kp + u) * 128:(2 * kp + u + 1) * 128],
                                                        rhs=QT[:, h, cols(qb)], start=True, stop=True) for u in range(2)],
                         reads=kres + [("QT", h, qb)], writes=sres)
                    pt, ptres = PT_r.next()
                    own = (kp - 2 * qb) if kp < 8 else -1
                    if own in (0, 1):
                        fns = []
                        for half in range(2):
                            hsl = slice(half * 256, (half + 1) * 256)
                            if half == own:
                                fns.append(lambda e, hsl=hsl: e.activation(out=pt[:, :, hsl], in_=sp2[:, :, hsl], func=AF.Exp,
                                                                           scale=scale))
                            else:
                                fns.append(lambda e, hsl=hsl: e.activation(out=pt[:, :, hsl], in_=sp2[:, :, hsl], func=AF.Exp,
                                                                           bias=negb[:, 0:1], scale=scale))
                        S.op("act", fns, reads=sres + ["negb"], writes=[ptres])
                    else:
                        S.op("act", lambda e: e.activation(out=pt, in_=sp2, func=AF.Exp, bias=negb[:, 0:1], scale=scale),
                             reads=sres + ["negb"], writes=[ptres])
                    tpb, tpres = tp_r.next()
                    S.op("dve", lambda e: e.tensor_tensor(out=tpb, in0=pt[:, 0, :], in1=pt[:, 1, :], op=ALU.add),
                         reads=[ptres], writes=[tpres])
                    if kp == 0:
                        accs[(qb, h)] = acc_r.next()
                    acc, accres = accs[(qb, h)]
                    if kp % 2 == 0:
                        hold[(qb, h)] = (tpb, tpres)
                    else:
                        tpa, tpares = hold.pop((qb, h))
                        S.op("dve", lambda e: e.tensor_tensor(out=tpb, in0=tpa, in1=tpb, op=ALU.add),
                             reads=[tpares, tpres], writes=[tpres])
                        if kp == 1:
                            S.op("dve", lambda e: e.tensor_copy(out=acc, in_=tpb), reads=[tpres], writes=[accres])
                        else:
                            S.op("dve", lambda e: e.tensor_tensor(out=acc, in0=acc, in1=tpb, op=ALU.add),
                                 reads=[tpres, accres], writes=[accres])
                    pend[i] = (pt, ptres)

                def emit_PV(i):
                    qb, h, kp = pairs[i]
                    kvh = h // 4
                    g = (qb * NCH + h) % 2
                    ops_, ores = PS(4 + g)
                    pt, ptres = pend.pop(i)
                    vres_ = [("Vb", 2 * kp, kvh), ("Vb", 2 * kp + 1, kvh)] if kp < 8 else ["Vc"]
                    S.op("pe", [lambda e, u=u: e.matmul(ops_, lhsT=Vb[:, 2 * kp + u, kvh * 128:(kvh + 1) * 128], rhs=pt[:, u, :],
                                                        start=(kp == 0 and u == 0), stop=(kp == NKP - 1 and u == 1))
                                for u in range(2)],
                         reads=[ptres] + vres_, writes=[ores])
                    if kp == NKP - 1:
                        acc, accres = accs.pop((qb, h))
                        sums, sures = PS(6)
                        lnr, lnres = lnr_r.next()
                        rec, recres = rec_r.next()

                        def t_sum():
                            S.op("pe", lambda e: e.matmul(sums, lhsT=ones_f[:], rhs=acc, start=True, stop=True),
                                 reads=[accres, "ones_f"], writes=[sures])

                        def t_rec():
                            S.op("act", lambda e: e.activation(out=lnr, in_=sums, func=AF.Ln), reads=[sures], writes=[lnres])
                            S.op("act", lambda e: e.activation(out=rec, in_=lnr, func=AF.Exp, scale=-1.0),
                                 reads=[lnres], writes=[recres])

                        def t_fin():
                            S.op("dve", lambda e: e.tensor_tensor(out=QT[:, h, cols(qb)], in0=ops_, in1=rec, op=ALU.mult),
                                 reads=[ores, recres, ("QT", h, qb)], writes=[("AT", h, qb)])
                        sched_at(i + 2, t_sum)
                        sched_at(i + 3, t_rec)
                        sched_at(i + 5, t_fin)

                def sched_wo(qb, base):
                    c = cols(qb)
                    ys = [yT[:, m, :] for m in range(NCH)]
                    yres_l = [("yT", m) for m in range(NCH)]
                    mps, mres = PS(7)
                    st_ = {}

                    def wo_mm(m):
                        yps, yres = PS(7)
                        S.op("pe", [lambda e, hh=hh: e.matmul(yps, lhsT=WO[:, hh, m * 128:(m + 1) * 128],
                                                              rhs=QT[:, hh, c], start=(hh == 0), stop=(hh == NCH - 1))
                                    for hh in range(NCH)],
                             reads=[("WO", hh) for hh in range(NCH)] + [("AT", hh, qb) for hh in range(NCH)], writes=[yres])

                    def wo_cp(m):
                        yps, yres = PS(7)
                        S.op("dve", lambda e: e.tensor_copy(out=yT[:, m, :], in_=yps), reads=[yres], writes=[("yT", m)])

                    def sq_a(k):
                        sq, sqres = NT.sq.next()
                        st_[("sq", k)] = (sq, sqres)
                        S.op("act", lambda e: e.activation(out=sq, in_=ys[k], func=AF.Square), reads=[yres_l[k]], writes=[sqres])

                    def sq_b(k):
                        sq, sqres = st_.pop(("sq", k))
                        S.op("pe", lambda e: e.matmul(mps, lhsT=ones_n[:], rhs=sq, start=(k == 0), stop=(k == NCH - 1)),
                             reads=[sqres, "ones_n"], writes=[mres])

                    def rstd_piece():
                        r, rres = NT.r.next()
                        S.op("act", lambda e: e.activation(out=r, in_=mps, func=AF.Ln, bias=EPS, scale=1.0),
                             reads=[mres], writes=[rres])
                        rstd, rsres = NT.rstd.next()
                        S.op("act", lambda e: e.activation(out=rstd, in_=r, func=AF.Exp, scale=-0.5), reads=[rres], writes=[rsres])
                        st_["rstd"] = (rstd, rsres)

                    def upd_piece(k):
                        rstd, rsres = st_["rstd"]
                        tmp, tres = NT.tmp.next()
                        S.op("dve", lambda e: e.tensor_tensor(out=tmp, in0=ys[k], in1=rstd, op=ALU.mult),
                             reads=[yres_l[k], rsres], writes=[tres])
                        S.op("dve", lambda e: e.scalar_tensor_tensor(
                            out=xT[:, k, c], in0=tmp, scalar=G1[:, k:k + 1], in1=xT[:, k, c], op0=ALU.mult, op1=ALU.add),
                             reads=[tres, ("x", k, qb)] + list(vr), writes=[("x", k, qb)])
                    for m in range(NCH):
                        sched_at(base + 2 * m, lambda m=m: wo_mm(m))
                        sched_at(base + 2 * m + 1, lambda m=m: wo_cp(m))
                    b2 = base + 2 * NCH + 2
                    for k in range(NCH):
                        sched_at(b2 + 2 * k, lambda k=k: sq_a(k))
                        sched_at(b2 + 2 * k + 1, lambda k=k: sq_b(k))
                    b3 = b2 + 2 * NCH + 1
                    sched_at(b3, rstd_piece)
                    for k in range(NCH):
                        sched_at(b3 + 2 + k, lambda k=k: upd_piece(k))

                n = len(pairs)
                per_qb = NCH * NKP
                due = {}
                seq_ = [0]

                def sched_at(it, fn):
                    due.setdefault(it, []).append(fn)

                emit_S(0)
                emit_S(1)
                for i in range(n):
                    if i + 2 < n:
                        emit_S(i + 2)
                    emit_PV(i)
                    if (i + 1) % per_qb == 0:
                        sched_wo((i + 1) // per_qb - 1, i + 8)
                    for fn in due.pop(i, []):
                        fn()
                for it in sorted(due):
                    for fn in due[it]:
                        fn()
            else:
                p = l // 2
                phase_barrier()
                AR.reset()
                hc_r = Rot("hc", [AR.f32(8 * 272).rearrange("p (s c) -> p s c", s=8) for _ in range(2)])
                s1 = AR.f32(8 * 272).rearrange("p (s c) -> p s c", s=8)
                s2 = AR.f32(8 * 272).rearrange("p (s c) -> p s c", s=8)
                pooledT = AR.bf16(NCH * T).rearrange("p (k t) -> p k t", k=NCH)
                yT2 = [AR.f32(NCH * TB).rearrange("p (k t) -> p k t", k=NCH) for _ in range(2)]
                WP = AR.bf16(4 * 2 * 256).rearrange("p (g k n) -> p g k n", g=4, k=2)
                rstdb = AR.f32(T)
                invc_r = Rot("invc", [AR.f32(T) for _ in range(1)])
                tmpf = AR.f32(T)
                tmpe = AR.f32(64).rearrange("p (s c) -> p s c", s=8)
                NT = NormTemps()
                for g in range(4):
                    S.dma("pool", lambda e, g=g: e.dma_start(out=WP[:, g, :, :],
                                                             in_=w_pool[p, g].rearrange("(k p) n -> p k n", p=128)),
                          writes=[("WP", g)])
                for i in range(2):
                    hcb = hc_r.aps[i]
                    S.op("dve", lambda e, hcb=hcb: e.memset(hcb, 0.0), writes=[("hc", i)])
                for tb in range(NTB):
                    c = cols(tb)
                    rstd, rsres = mean_rstd(NT, [xT[:, k, c] for k in range(NCH)], [("x", k, tb) for k in range(NCH)],
                                            tb % 2, ones_n, "ones_n")
                    S.op("dve", lambda e, c=c, rstd=rstd: e.tensor_copy(out=rstdb[:, c], in_=rstd),
                         reads=[rsres], writes=[("rstdb", tb)])
                invc = None
                for k in range(NCH):
                    g = k // 2
                    w = POOL_W[g]
                    if k % 2 == 0:
                        invc, invres = invc_r.next()
                        S.dma("sp", lambda e, invc=invc, g=g: e.dma_start(out=invc, in_=invcnt_in[g]), writes=[invres])
                    hc, hres = hc_r.next()
                    S.op("dve", lambda e, k=k: e.tensor_tensor(out=tmpf, in0=xT[:, k, :], in1=rstdb, op=ALU.mult),
                         reads=[("x", k, tb) for tb in range(NTB)] + [("rstdb", tb) for tb in range(NTB)], writes=["tmpf"])
                    S.op("dve", lambda e, hc=hc, k=k: e.tensor_scalar(
                        out=hc[:, :, 8:264], in0=tmpf.rearrange("p (s c) -> p s c", s=8), scalar1=A1[:, k:k + 1],
                        scalar2=B1[:, k:k + 1], op0=ALU.mult, op1=ALU.add), reads=["tmpf"] + vr, writes=[hres])
                    S.op("dve", lambda e, hc=hc: e.tensor_scalar(out=hc[:, 1:8, 0:8], in0=hc[:, 0:7, 256:264],
                                                                 scalar1=flag[:, 0:1], scalar2=None, op0=ALU.mult),
                         reads=[hres, "flag"], writes=[hres])
                    S.op("dve", lambda e, hc=hc: e.tensor_scalar(out=hc[:, 0:7, 264:272], in0=hc[:, 1:8, 8:16],
                                                                 scalar1=flag[:, 0:1], scalar2=None, op0=ALU.mult),
                         reads=[hres, "flag"], writes=[hres])
                    S.op("dve", lambda e, hc=hc: e.tensor_tensor(out=s1[:, :, 1:272], in0=hc[:, :, 0:271], in1=hc[:, :, 1:272],
                                                                 op=ALU.add), reads=[hres], writes=["s1"])
                    fin, finres = s1, "s1"
                    if w >= 4:
                        S.op("dve", lambda e: e.tensor_tensor(out=s2[:, :, 2:271], in0=s1[:, :, 1:270], in1=s1[:, :, 3:272],
                                                              op=ALU.add), reads=["s1"], writes=["s2"])
                        fin, finres = s2, "s2"
                    if w >= 8:
                        S.op("dve", lambda e: e.tensor_tensor(out=s1[:, :, 4:269], in0=s2[:, :, 2:267], in1=s2[:, :, 6:271],
                                                              op=ALU.add), reads=["s2"], writes=["s1"])
                        fin, finres = s1, "s1"
                    if w >= 16:
                        S.op("dve", lambda e: e.tensor_tensor(out=s2[:, :, 8:264], in0=s1[:, :, 4:260], in1=s1[:, :, 12:268],
                                                              op=ALU.add), reads=["s1"], writes=["s2"])
                        fin, finres = s2, "s2"
                    pv = pooledT[:, k, :].rearrange("p (s c) -> p s c", s=8)
                    iv = invc.rearrange("p (s c) -> p s c", s=8)
                    S.op("dve", lambda e, fin=fin, hc=hc, pv=pv, w=w: e.scalar_tensor_tensor(
                        out=pv, in0=fin[:, :, 8:264], scalar=1.0 / w, in1=hc[:, :, 8:264], op0=ALU.mult, op1=ALU.subtract),
                         reads=[finres, hres], writes=[("pooled", k)])
                    for (c0, c1) in ((0, 8), (248, 256)):
                        S.op("dve", lambda e, fin=fin, iv=iv, c0=c0, c1=c1: e.tensor_tensor(
                            out=tmpe, in0=fin[:, :, 8 + c0:8 + c1], in1=iv[:, :, c0:c1], op=ALU.mult),
                             reads=[finres, invres], writes=["tmpe"])
                        S.op("dve", lambda e, hc=hc, pv=pv, c0=c0, c1=c1: e.tensor_tensor(
                            out=pv[:, :, c0:c1], in0=tmpe, in1=hc[:, :, 8 + c0:8 + c1], op=ALU.subtract),
                             reads=["tmpe", hres, ("pooled", k)], writes=[("pooled", k)])
                    if DEBUG and l == 1 and k in (0, 7):
                        di = 0 if k == 0 else 1
                        S.dma("sp", lambda e, hc=hc, di=di: e.dma_start(out=dbg_hc[di], in_=hc.rearrange("p s c -> p (s c)")),
                              reads=[hres], key=("dbg", "hc", di))
                        S.dma("sp", lambda e, fin=fin, di=di: e.dma_start(out=dbg_fin[di], in_=fin.rearrange("p s c -> p (s c)")),
                              reads=[finres], key=("dbg", "fin", di))
                        S.dma("sp", lambda e, di=di: e.dma_start(out=dbg_tmpf[di], in_=tmpf), reads=["tmpf"], key=("dbg", "tmpf", di))
                        S.dma("pool", lambda e, di=di, k=k: e.dma_start(out=dbg_pooled[di], in_=pooledT[:, k, :]),
                              reads=[("pooled", k)], key=("dbg", "pooled", di))
                yi = 0

                def pool_mm(tb):
                    nonlocal yi
                    c = cols(tb)
                    yT = yT2[tb % 2]
                    for m in range(NCH):
                        g, oc = m // 2, m % 2
                        yps, yres = PS(2 + yi % 2)
                        yi += 1
                        S.op("pe", [lambda e, kc=kc, g=g, oc=oc, yps=yps, c=c: e.matmul(
                            yps, lhsT=WP[:, g, kc, oc * 128:(oc + 1) * 128], rhs=pooledT[:, 2 * g + kc, c],
                            start=(kc == 0), stop=(kc == 1)) for kc in range(2)],
                             reads=[("WP", g), ("pooled", 2 * g), ("pooled", 2 * g + 1)], writes=[yres])
                        S.op("act", lambda e, m=m, yps=yps, yT=yT: e.activation(out=yT[:, m, :], in_=yps, func=AF.Identity,
                                                                                scale=pscT[:, p * 8 + m:p * 8 + m + 1]),
                             reads=[yres, "pscT"], writes=[("yT", tb % 2, m)])
                        if pside:
                            for fn in pside.pop(0):
                                fn()

                def pool_post(tb):
                    yT = yT2[tb % 2]
                    post_norm(NT, tb, 4, [yT[:, m, :] for m in range(NCH)], [("yT", tb % 2, m) for m in range(NCH)], G1, vr)

                pside = []
                pool_mm(0)
                for tb in range(NTB):
                    yTb = yT2[tb % 2]
                    pside.extend(norm_groups(NT, tb, 4, [yTb[:, m, :] for m in range(NCH)],
                                             [("yT", tb % 2, m) for m in range(NCH)], "post", Gv=G1, vres=vr))
                    if tb + 1 < NTB:
                        pool_mm(tb + 1)
                    while pside:
                        for fn in pside.pop(0):
                            fn()

            phase_barrier()
            AR.reset()
            HB = 2 * TB
            hTh = AR.bf16(NCH * HB).rearrange("p (k t) -> p k t", k=NCH)
            yTh = AR.f32(NCH * HB).rearrange("p (k t) -> p k t", k=NCH)
            hmid = AR.bf16(NF * HB).rearrange("p (j t) -> p j t", j=NF)
            WU = Rot("wu", [AR.bf16(NCH * 256).rearrange("p (k n) -> p k n", k=NCH) for _ in range(3)])
            WD = Rot("wd", [AR.bf16(NF * 128).rearrange("p (j n) -> p j n", j=NF) for _ in range(2)])
            WA = Rot("wa", [AR.bf16(8 * 256).rearrange("p (k n) -> p k n", k=8) for _ in range(2)])
            sg_r = Rot("sg", [AR.f32(TB) for _ in range(2)])
            NT = NormTemps()
            wup_v = w_up[l].rearrange("(k p) n -> p k n", p=128)
            wdn_v = w_down[l].rearrange("(j p) n -> p j n", p=128)
            do_ada = (l + 1 < n_layers)
            adaps, adares = PS(7)
            ada_s = 0
            ui = 0
            side = []

            def side_step():
                if side:
                    for fn in side.pop(0):
                        fn()

            def side_flush():
                while side:
                    side_step()

            def pre_groups(hf):
                gs = []
                for tt in range(2):
                    tb = 2 * hf + tt
                    gs += norm_groups(NT, tb, 6, [xT[:, k, cols(tb)] for k in range(NCH)], [("x", k, tb) for k in range(NCH)],
                                      "pre", Av=A2, Bv=B2, vres=vr,
                                      outs=[hTh[:, k, tt * TB:(tt + 1) * TB] for k in range(NCH)],
                                      out_res=[("hTh", k, tt) for k in range(NCH)])
                return gs

            def post_groups(hf):
                gs = []
                for tt in range(2):
                    hs = slice(tt * TB, (tt + 1) * TB)
                    gs += norm_groups(NT, 2 * hf + tt, 6, [yTh[:, m, hs] for m in range(NCH)],
                                      [("yTh", m, tt) for m in range(NCH)], "post", Gv=G2, vres=vr)
                return gs

            def ffn_pre(hf):
                nonlocal ui, ada_s
                for tt in range(2):
                    tb = 2 * hf + tt
                    pre_norm(NT, tb, 6, A2, B2, vr, [hTh[:, k, tt * TB:(tt + 1) * TB] for k in range(NCH)],
                             [("hTh", k, tt) for k in range(NCH)])
            def ffn_up(hf):
                nonlocal ui, ada_s
                order = [(j, tt) for j in range(NF) for tt in range(2)]
                if hf == 0:
                    order = [(j, 0) for j in range(3)] + [(j, 1) for j in range(3)] + order[6:]
                wu_of = {}
                for (j, tt) in order:
                    if j not in wu_of:
                        wu, wures = WU.next()
                        S.dma("pool", lambda e, wu=wu, j=j: e.dma_start(out=wu[:, :, 0:128],
                                                                         in_=wup_v[:, :, j * 128:(j + 1) * 128]),
                              writes=[wures], key=("wu", WU.i, 0))
                        S.dma("pool", lambda e, wu=wu, j=j: e.dma_start(out=wu[:, :, 128:256],
                                                                         in_=wup_v[:, :, DFF + j * 128:DFF + (j + 1) * 128]),
                              writes=[(wures, "u")], key=("wu", WU.i, 1))
                        wu_of[j] = (wu, wures)
                    wu, wures = wu_of[j]
                    if True:
                        gps, gres = PS(ui % 2)
                        ups, ures = PS(2 + ui % 2)
                        ui += 1
                        hs = slice(tt * TB, (tt + 1) * TB)
                        S.op("pe", [lambda e, k=k, wu=wu, gps=gps, hs=hs: e.matmul(
                            gps, lhsT=wu[:, k, 0:128], rhs=hTh[:, k, hs], start=(k == 0), stop=(k == NCH - 1))
                            for k in range(NCH)],
                             reads=[wures] + [("hTh", k, tt) for k in range(NCH)], writes=[gres])
                        S.op("pe", [lambda e, k=k, wu=wu, ups=ups, hs=hs: e.matmul(
                            ups, lhsT=wu[:, k, 128:256], rhs=hTh[:, k, hs], start=(k == 0), stop=(k == NCH - 1))
                            for k in range(NCH)],
                             reads=[(wures, "u")] + [("hTh", k, tt) for k in range(NCH)], writes=[ures])
                        sg, sgres = sg_r.next()
                        S.op("act", lambda e, sg=sg, gps=gps: e.activation(out=sg, in_=gps, func=AF.Silu),
                             reads=[gres], writes=[sgres])
                        S.op("dve", lambda e, sg=sg, ups=ups, j=j, hs=hs: e.tensor_tensor(
                            out=hmid[:, j, hs], in0=sg, in1=ups, op=ALU.mult),
                             reads=[sgres, ures], writes=[("hmid", j, tt)])
                        if j >= 1:
                            side_step()
                    if tt == 1 and do_ada and ada_s < 24:
                        emit_ada_slice(l + 1, ada_s, WA, adaps, adares)
                        ada_s += 1
            def ffn_down(hf):
                nonlocal ui, ada_s
                for m in range(NCH):
                    wd, wdres = WD.next()
                    S.dma("pool", lambda e, wd=wd, m=m: e.dma_start(out=wd, in_=wdn_v[:, :, m * 128:(m + 1) * 128]),
                          writes=[wdres])
                    for tt in range(2):
                        yps, yres = PS(4 + (2 * m + tt) % 2)
                        hs = slice(tt * TB, (tt + 1) * TB)
                        S.op("pe", [lambda e, j=j, wd=wd, yps=yps, hs=hs: e.matmul(
                            yps, lhsT=wd[:, j, :], rhs=hmid[:, j, hs], start=(j == 0), stop=(j == NF - 1))
                            for j in range(NF)],
                             reads=[wdres] + [("hmid", j, tt) for j in range(NF)], writes=[yres])
                        S.op("act", lambda e, m=m, yps=yps, hs=hs: e.activation(out=yTh[:, m, hs], in_=yps, func=AF.Copy),
                             reads=[yres], writes=[("yTh", m, tt)])
                        side_step()
            def ffn_post(hf):
                nonlocal ui, ada_s
                for tt in range(2):
                    hs = slice(tt * TB, (tt + 1) * TB)
                    post_norm(NT, 2 * hf + tt, 6, [yTh[:, m, hs] for m in range(NCH)],
                              [("yTh", m, tt) for m in range(NCH)], G2, vr)
            def ffn_down_last(hf):
                st_ = {}
                mp = {0: PS(6), 1: PS(0)}

                def sq_a(m, tt):
                    hs = slice(tt * TB, (tt + 1) * TB)
                    sq, sqres = NT.sq.next()
                    st_[(m, tt)] = (sq, sqres)
                    S.op("act", lambda e: e.activation(out=sq, in_=yTh[:, m, hs], func=AF.Square),
                         reads=[("yTh", m, tt)], writes=[sqres])

                def sq_b(m, tt):
                    sq, sqres = st_.pop((m, tt))
                    mps, mres = mp[tt]
                    S.op("pe", lambda e: e.matmul(mps, lhsT=ones_n[:], rhs=sq, start=(m == 0), stop=(m == NCH - 1)),
                         reads=[sqres, "ones_n"], writes=[mres])

                for m in range(NCH):
                    wd, wdres = WD.next()
                    S.dma("pool", lambda e, wd=wd, m=m: e.dma_start(out=wd, in_=wdn_v[:, :, m * 128:(m + 1) * 128]),
                          writes=[wdres])
                    for tt in range(2):
                        yps, yres = PS(4 + (2 * m + tt) % 2)
                        hs = slice(tt * TB, (tt + 1) * TB)
                        S.op("pe", [lambda e, j=j, wd=wd, yps=yps, hs=hs: e.matmul(
                            yps, lhsT=wd[:, j, :], rhs=hmid[:, j, hs], start=(j == 0), stop=(j == NF - 1))
                            for j in range(NF)],
                             reads=[wdres] + [("hmid", j, tt) for j in range(NF)], writes=[yres])
                        S.op("act", lambda e, m=m, yps=yps, hs=hs: e.activation(out=yTh[:, m, hs], in_=yps, func=AF.Copy),
                             reads=[yres], writes=[("yTh", m, tt)])
                    if m >= 1:
                        sq_b(m - 1, 0)
                        sq_b(m - 1, 1)
                    sq_a(m, 0)
                    sq_a(m, 1)
                sq_b(NCH - 1, 0)
                sq_b(NCH - 1, 1)
                rs = {}
                for tt in range(2):
                    mps, mres = mp[tt]
                    r, rres = NT.r.next()
                    S.op("act", lambda e, r=r, mps=mps: e.activation(out=r, in_=mps, func=AF.Ln, bias=EPS, scale=1.0),
                         reads=[mres], writes=[rres])
                    rstd, rsres = NT.rstd.next()
                    S.op("act", lambda e, r=r, rstd=rstd: e.activation(out=rstd, in_=r, func=AF.Exp, scale=-0.5),
                         reads=[rres], writes=[rsres])
                    rs[tt] = (rstd, rsres)
                for tt in range(2):
                    hs = slice(tt * TB, (tt + 1) * TB)
                    tb = 2 * hf + tt
                    c = cols(tb)
                    rstd, rsres = rs[tt]
                    for k in range(NCH):
                        tmp, tres = NT.tmp.next()
                        S.op("dve", lambda e, tmp=tmp, k=k, hs=hs, rstd=rstd: e.tensor_tensor(
                            out=tmp, in0=yTh[:, k, hs], in1=rstd, op=ALU.mult), reads=[("yTh", k, tt), rsres], writes=[tres])
                        S.op("dve", lambda e, tmp=tmp, k=k, c=c: e.scalar_tensor_tensor(
                            out=xT[:, k, c], in0=tmp, scalar=G2[:, k:k + 1], in1=xT[:, k, c], op0=ALU.mult, op1=ALU.add),
                             reads=[tres, ("x", k, tb)] + list(vr), writes=[("x", k, tb)])

            ffn_pre(0)
            ffn_up(0)
            side.extend(pre_groups(1))
            ffn_down(0)
            side_flush()
            side.extend(post_groups(0))
            ffn_up(1)
            side_flush()
            ffn_down_last(1)
            if do_ada:
                while ada_s < 24:
                    emit_ada_slice(l + 1, ada_s, WA, adaps, adares)
                    ada_s += 1
                finish_ada(l + 1, adaps, adares)
            if DEBUG and l == 0:
                S.dma("sp", lambda e: e.dma_start(out=dbg_x, in_=xT.rearrange("p k t -> p (k t)")),
                      reads=[("x", k, tb) for k in range(NCH) for tb in range(NTB)], key=("dbgx", 0))

        for l_ in range(n_layers):
            layer_body(l_)

        yout_v = yT_out.rearrange("(k p) t -> p k t", p=128)
        for tb in range(NTB):
            c = cols(tb)
            for k in range(NCH):
                S.dma("sp", lambda e, c=c, k=k: e.dma_start(out=yout_v[:, k, c], in_=xT[:, k, c]),
                      reads=[("x", k, tb)], key=("xout", k), phase=False)
        S.final_wait("sp")
        S.emit()
    return nc


def _role_tables(is_sample):
    rope = np.zeros((2, 128, T), np.float32)
    if is_sample:
        t = np.arange(T)
        row = (t // 64).astype(np.float32)
        col = (t % 64).astype(np.float32)
        inv = (10000.0 ** (-np.arange(32, dtype=np.float32) / 32)).astype(np.float32)
        ang_r = row[None, :] * inv[:, None]
        ang_c = col[None, :] * inv[:, None]
        for base, ang in ((0, ang_r), (64, ang_c)):
            rope[0, base:base + 32] = np.cos(ang)
            rope[0, base + 32:base + 64] = np.cos(ang)
            rope[1, base:base + 32] = -np.sin(ang)
            rope[1, base + 32:base + 64] = np.sin(ang)
    else:
        rope[0] = 1.0
    mask = np.zeros((128, 3, TB), np.float32)
    if is_sample:
        mask[:] = 1.0
    else:
        mask[:, 0, 0:256] = 1.0
        mask[:, 1, 256:512] = 1.0
    flag = np.full((128, 1), 1.0 if is_sample else 0.0, np.float32)
    negb = np.full((128, 1), 0.0 if is_sample else -30000.0, np.float32)
    L = T if is_sample else 256
    pos = np.arange(T) % L
    invcnt = np.zeros((4, 128, T), np.float32)
    for g, w in enumerate(POOL_W):
        lo = np.clip(pos - w // 2, 0, L)
        hi = np.clip(pos + w - w // 2, 0, L)
        invcnt[g] = (1.0 / (hi - lo).astype(np.float32))[None, :]
    return rope, mask.reshape(128, 3 * TB), flag, invcnt, negb


def _fm(v):
    v = np.asarray(v, np.float32)
    lead = v.shape[:-1]
    return np.ascontiguousarray(np.moveaxis(v.reshape(lead + (8, 128)), -1, 0))


_NC_CACHE = {}


def kernel(x_prompt, x_sample, c, cache_k, cache_v, c_ctx, w_ada, b_ada, norm_gains,
           w_qkv, qk_gains, w_o, w_pool, pool_scale, w_up, w_down, _n_layers=DEPTH):
    f = lambda a: np.ascontiguousarray(np.asarray(a, dtype=np.float32))
    x_prompt, x_sample, c, cache_k, cache_v, c_ctx = map(f, (x_prompt, x_sample, c, cache_k, cache_v, c_ctx))
    w_ada, b_ada, norm_gains, w_qkv, qk_gains, w_o, w_pool, pool_scale, w_up, w_down = map(
        f, (w_ada, b_ada, norm_gains, w_qkv, qk_gains, w_o, w_pool, pool_scale, w_up, w_down))
    if _n_layers not in _NC_CACHE:
        _NC_CACHE[_n_layers] = build_program(_n_layers)
    nc = _NC_CACHE[_n_layers]

    badaT = np.ascontiguousarray(b_ada.reshape(DEPTH, 48, 128).transpose(2, 0, 1).reshape(128, DEPTH * 48))
    gT = _fm(norm_gains).reshape(128, DEPTH * 4 * 8)
    qkgT = np.ascontiguousarray(qk_gains.reshape(4, 128).T)
    pscT = _fm(pool_scale).reshape(128, 16)
    rperm = np.zeros((128, 128), np.float32)
    for d in range(128):
        rperm[d ^ 32, d] = 1.0
    shared = dict(w_ada=w_ada, badaT=badaT, gT=gT, w_qkv=w_qkv, qkgT=qkgT, w_o=w_o, w_pool=w_pool,
                  pscT=pscT, w_up=w_up, w_down=w_down, rperm=rperm)
    tabs = {False: _role_tables(False), True: _role_tables(True)}
    zeroKT = np.zeros((2, 2, 128, PAST), np.float32)
    zeroV = np.zeros((2, PAST, 256), np.float32)
    in_maps = []
    for core in range(8):
        is_s = core >= 4
        rope, mask, flag, invcnt, negb = tabs[is_s]
        m = dict(shared)
        if is_s:
            b = core - 4
            m["xT_in"] = np.ascontiguousarray(x_sample[b].T)
            m["cond"] = _fm(c[b])
            m["cacheKT"] = np.ascontiguousarray(cache_k[b].transpose(0, 2, 3, 1))
            m["cacheV"] = np.ascontiguousarray(cache_v[b].reshape(2, PAST, 256))
        else:
            xs = x_prompt[core * 8:(core + 1) * 8].reshape(8 * 256, D)
            m["xT_in"] = np.ascontiguousarray(xs.T)
            m["cond"] = _fm(c_ctx)
            m["cacheKT"] = zeroKT
            m["cacheV"] = zeroV
        m["rope"], m["maskT"], m["flag"], m["invcnt"], m["negb"] = rope, mask, flag, invcnt, negb
        in_maps.append(m)
    res = run_bass_kernel_spmd(nc, in_maps, core_ids=list(range(8)))
    R = res.results
    if DEBUG:
        DBG["res"] = R
    y_prompt = np.empty((32, 256, D), np.float32)
    y_sample = np.empty((4, T, D), np.float32)
    new_k = np.empty((32, 2, 256, 2, 128), np.float32)
    new_v = np.empty((32, 2, 256, 2, 128), np.float32)
    for core in range(8):
        yT = np.asarray(R[core]["yT_out"])
        if core >= 4:
            y_sample[core - 4] = yT.T
        else:
            y_prompt[core * 8:(core + 1) * 8] = yT.T.reshape(8, 256, D)
            nk = np.asarray(R[core]["newkT"]).reshape(2, 2, 128, 8, 256)
            new_k[core * 8:(core + 1) * 8] = nk.transpose(3, 0, 4, 1, 2)
            nv = np.asarray(R[core]["newv"]).reshape(2, 8, 256, 2, 128)
            new_v[core * 8:(core + 1) * 8] = nv.transpose(1, 0, 2, 3, 4)
    return (y_prompt, y_sample, new_k, new_v)
```
